# Optimizing a Trainium2 kernel written in Bass

```python
import jax, jax.numpy as jnp
from jax import lax
import numpy as np

D_MODEL = 2048
BATCH = 8
SEQ = 4096
DEPTH = 4

N_META = 16
BLOCK_Q = 128
ATTN_HEAD_DIM = 128
ATTN_WIDTH = D_MODEL // 2
ATTN_HEADS = ATTN_WIDTH // ATTN_HEAD_DIM
LRU_WIDTH = D_MODEL - ATTN_WIDTH
LRU_HEADS = 8
LRU_HEAD_DIM = LRU_WIDTH // LRU_HEADS
MIX_WIDTH = ATTN_WIDTH + LRU_WIDTH
LRU_CONV = 4
LRU_C = 8.0
IN_COLS = 3 * ATTN_WIDTH + 2 * LRU_WIDTH
FFN_HIDDEN = 5632
FFN_CONV = 3
DEEPNORM_ALPHA = (2 * DEPTH) ** 0.25
DEEPNORM_BETA = (8 * DEPTH) ** -0.25
LN_EPS = 1e-5
RMS_EPS = 1e-6

kernel_name = "hymba_stickbreak_rglru_deepnorm_trunk"


def layer_norm(x, g, b):
    xf = x.astype(jnp.float32)
    mu = jnp.mean(xf, axis=-1, keepdims=True)
    var = jnp.mean(jnp.square(xf - mu), axis=-1, keepdims=True)
    return ((xf - mu) * lax.rsqrt(var + LN_EPS) * g + b).astype(x.dtype)


def rms_norm(x, g):
    xf = x.astype(jnp.float32)
    ms = jnp.mean(jnp.square(xf), axis=-1, keepdims=True)
    return (xf * lax.rsqrt(ms + RMS_EPS) * g).astype(x.dtype)


def causal_dwconv(x, w, b):
    K = w.shape[0]
    T = x.shape[1]
    xp = jnp.pad(x, ((0, 0), (K - 1, 0), (0, 0)))
    y = xp[:, 0:T] * w[0]
    for tap in range(1, K):
        y = y + xp[:, tap:tap + T] * w[tap]
    return y + b


def stick_breaking_attention(q, k, v, n_pad):
    L = q.shape[1]
    scale = ATTN_HEAD_DIM ** -0.5
    outs = []
    for blk in range(L // BLOCK_Q):
        q0 = blk * BLOCK_Q
        kend = q0 + BLOCK_Q
        z = jnp.einsum('bqhd,bkhd->bhqk', q[:, q0:kend], k[:, :kend]).astype(jnp.float32) * scale
        t_idx = q0 + jnp.arange(BLOCK_Q)[:, None]
        s_idx = jnp.arange(kend)[None, :]
        valid = (s_idx < t_idx) & (s_idx >= n_pad)
        log_1m_beta = jnp.where(valid, jax.nn.log_sigmoid(-z), 0.0)
        later = lax.cumsum(log_1m_beta, axis=3, reverse=True) - log_1m_beta
        w = jnp.where(valid, jnp.exp(jax.nn.log_sigmoid(z) + later), 0.0)
        outs.append(jnp.einsum('bhqk,bkhd->bqhd', w.astype(v.dtype), v[:, :kend]))
    return jnp.concatenate(outs, axis=1)


def rg_lru(x, w_a, b_a, w_i, b_i, lam):
    B, T, C = x.shape
    xh = x.reshape(B, T, LRU_HEADS, LRU_HEAD_DIM)
    r = jax.nn.sigmoid(jnp.einsum('bthi,hij->bthj', xh, w_a).reshape(B, T, C) + b_a)
    i = jax.nn.sigmoid(jnp.einsum('bthi,hij->bthj', xh, w_i).reshape(B, T, C) + b_i)
    log_a = (-LRU_C * r * jax.nn.softplus(-lam)).astype(jnp.float32)
    a = jnp.exp(log_a)
    u = jnp.sqrt(-jnp.expm1(2.0 * log_a)) * (i * x).astype(jnp.float32)

    def combine(c1, c2):
        a1, b1 = c1
        a2, b2 = c2
        return a1 * a2, a2 * b1 + b2

    _, h = lax.associative_scan(combine, (a, u), axis=1)
    return h.astype(x.dtype)


def hybrid_mixer(x, w_in, lru_conv_w, lru_conv_b, lru_w_a, lru_b_a, lru_w_i, lru_b_i,
                 lru_lambda, g_attn, g_lru, w_out):
    B, T, _ = x.shape
    proj = x @ w_in
    q, k, v, xr, yg = jnp.split(
        proj, [ATTN_WIDTH, 2 * ATTN_WIDTH, 3 * ATTN_WIDTH, 3 * ATTN_WIDTH + LRU_WIDTH], axis=-1)
    n_pad = (-T) % BLOCK_Q
    pad = ((0, 0), (n_pad, 0), (0, 0), (0, 0))
    heads = lambda t: jnp.pad(t.reshape(B, T, ATTN_HEADS, ATTN_HEAD_DIM), pad)
    o_attn = stick_breaking_attention(heads(q), heads(k), heads(v), n_pad)[:, n_pad:]
    o_attn = o_attn.reshape(B, T, ATTN_WIDTH)
    xc = causal_dwconv(xr, lru_conv_w, lru_conv_b)
    o_lru = rg_lru(xc, lru_w_a, lru_b_a, lru_w_i, lru_b_i, lru_lambda) * jax.nn.gelu(yg)
    merged = jnp.concatenate([rms_norm(o_attn, g_attn), rms_norm(o_lru, g_lru)], axis=-1)
    return merged @ w_out


def conv_glu_ffn(x, w_up, conv_w, conv_b, w_down):
    u = causal_dwconv(x @ w_up, conv_w, conv_b)
    g, val = jnp.split(u, 2, axis=-1)
    return (jax.nn.silu(g) * val) @ w_down


def setup_inputs(seed: int = 0) -> dict:
    key = jax.random.key(seed)
    ks = jax.random.split(key, 24)
    f32 = jnp.float32
    nrm = lambda kk, shape, s: jax.random.normal(kk, shape, f32) * s
    base = jax.random.uniform(ks[9], (DEPTH, LRU_WIDTH), f32, 0.9, 0.999) ** (1.0 / LRU_C)
    return {
        "x": nrm(ks[0], (BATCH, SEQ, D_MODEL), 1.0),
        "meta_tokens": nrm(ks[1], (N_META, D_MODEL), 1.0),
        "ln0_g": 1.0 + nrm(ks[2], (D_MODEL,), 0.02),
        "ln0_b": nrm(ks[3], (D_MODEL,), 0.02),
        "w_in": nrm(ks[4], (DEPTH, D_MODEL, IN_COLS), D_MODEL ** -0.5),
        "lru_conv_w": nrm(ks[5], (DEPTH, LRU_CONV, LRU_WIDTH), LRU_CONV ** -0.5),
        "lru_conv_b": nrm(ks[6], (DEPTH, LRU_WIDTH), 0.01),
        "lru_w_a": nrm(ks[7], (DEPTH, LRU_HEADS, LRU_HEAD_DIM, LRU_HEAD_DIM), LRU_HEAD_DIM ** -0.5),
        "lru_b_a": nrm(ks[8], (DEPTH, LRU_WIDTH), 0.01),
        "lru_w_i": nrm(ks[10], (DEPTH, LRU_HEADS, LRU_HEAD_DIM, LRU_HEAD_DIM), LRU_HEAD_DIM ** -0.5),
        "lru_b_i": nrm(ks[11], (DEPTH, LRU_WIDTH), 0.01),
        "lru_lambda": jnp.log(base) - jnp.log1p(-base),
        "g_attn": 1.0 + nrm(ks[12], (DEPTH, ATTN_WIDTH), 0.02),
        "g_lru": 1.0 + nrm(ks[13], (DEPTH, LRU_WIDTH), 0.02),
        "w_out": nrm(ks[14], (DEPTH, MIX_WIDTH, D_MODEL), DEEPNORM_BETA * MIX_WIDTH ** -0.5),
        "ln1_g": 1.0 + nrm(ks[15], (DEPTH, D_MODEL), 0.02),
        "ln1_b": nrm(ks[16], (DEPTH, D_MODEL), 0.02),
        "ffn_w_up": nrm(ks[17], (DEPTH, D_MODEL, 2 * FFN_HIDDEN), D_MODEL ** -0.5),
        "ffn_conv_w": nrm(ks[18], (DEPTH, FFN_CONV, 2 * FFN_HIDDEN), FFN_CONV ** -0.5),
        "ffn_conv_b": nrm(ks[19], (DEPTH, 2 * FFN_HIDDEN), 0.01),
        "ffn_w_down": nrm(ks[20], (DEPTH, FFN_HIDDEN, D_MODEL), DEEPNORM_BETA * FFN_HIDDEN ** -0.5),
        "ln2_g": 1.0 + nrm(ks[21], (DEPTH, D_MODEL), 0.02),
        "ln2_b": nrm(ks[22], (DEPTH, D_MODEL), 0.02),
    }


def reference(x, meta_tokens, ln0_g, ln0_b, w_in, lru_conv_w, lru_conv_b, lru_w_a, lru_b_a,
              lru_w_i, lru_b_i, lru_lambda, g_attn, g_lru, w_out, ln1_g, ln1_b,
              ffn_w_up, ffn_conv_w, ffn_conv_b, ffn_w_down, ln2_g, ln2_b):
    B = x.shape[0]
    meta = jnp.broadcast_to(meta_tokens[None].astype(x.dtype), (B, N_META, D_MODEL))
    h = layer_norm(jnp.concatenate([meta, x], axis=1), ln0_g, ln0_b)
    for l in range(DEPTH):
        mix = hybrid_mixer(h, w_in[l], lru_conv_w[l], lru_conv_b[l], lru_w_a[l], lru_b_a[l],
                           lru_w_i[l], lru_b_i[l], lru_lambda[l], g_attn[l], g_lru[l], w_out[l])
        h = layer_norm(DEEPNORM_ALPHA * h + mix, ln1_g[l], ln1_b[l])
        ffn = conv_glu_ffn(h, ffn_w_up[l], ffn_conv_w[l], ffn_conv_b[l], ffn_w_down[l])
        h = layer_norm(DEEPNORM_ALPHA * h + ffn, ln2_g[l], ln2_b[l])
    return h[:, N_META:]
```

```python
import numpy as np
from contextlib import ExitStack
import concourse.bass as bass
import concourse.mybir as mybir
from concourse.bass_utils import run_bass_kernel_spmd

F32 = mybir.dt.float32
BF16 = mybir.dt.bfloat16
AF = mybir.ActivationFunctionType
ALU = mybir.AluOpType

D = 2048
SEQ = 4096
NMETA = 16
DEPTH = 4
TP = 4224
NPAD = 112
NB = 33
KC = 16
H = 8
NIN = 40
FH = 5632
NHC = 44
ALPHA = float((2 * DEPTH) ** 0.25)
SCALE = float(128 ** -0.5)
LN_EPS = 1e-5
RMS_EPS = 1e-6
NV = 432
NWARM = 0
V_CW, V_CB, V_BA, V_BI, V_LAM, V_G, V_FCW, V_FCB = 0, 32, 40, 48, 56, 64, 80, 344


class Tok:
    __slots__ = ("w", "r", "sem", "cnt", "name")

    def __init__(self, name):
        self.w = {}
        self.r = {}
        self.sem = None
        self.cnt = 0
        self.name = name


class Sched:
    ENG = ("pe", "act", "dve", "pool", "sp")

    def __init__(self, nc, stack):
        self.nc = nc
        self.stack = stack
        self.ops = {e: [] for e in self.ENG}
        self.esem = {}
        self.ecnt = {e: 0 for e in self.ENG}
        self.waited = {e: {} for e in self.ENG}
        for e in self.ENG:
            self.esem[e] = stack.enter_context(nc.semaphore("s_" + e))
        self.nsem = 5
        self.toks = {}
        self.dtoks = []
        self.ninstr = {e: 0 for e in self.ENG}

    def tok(self, key):
        if isinstance(key, Tok):
            return key
        t = self.toks.get(key)
        if t is None:
            t = Tok(key)
            self.toks[key] = t
        return t

    def _deps(self, eng, reads, writes):
        deps = {}

        def upd(d):
            for k, sv in d.items():
                o = deps.get(k)
                if o is None or o[1] < sv[1]:
                    deps[k] = sv
        for t in reads:
            upd(t.w)
        for t in writes:
            upd(t.w)
            upd(t.r)
        out = []
        wd = self.waited[eng]
        for k, (s, v) in deps.items():
            if eng == "pe" and k == "pe":
                continue
            if wd.get(k, 0) >= v:
                continue
            wd[k] = v
            out.append((s, v))
        return out

    def _commit(self, key, sem, val, reads, writes):
        for t in reads:
            o = t.r.get(key)
            if o is None or o[1] < val:
                t.r[key] = (sem, val)
        for t in writes:
            t.w = {key: (sem, val)}
            t.r = {}

    def op(self, eng, fn, reads=(), writes=()):
        reads = [self.tok(k) for k in reads]
        writes = [self.tok(k) for k in writes]
        waits = self._deps(eng, reads, writes)
        self.ecnt[eng] += 1
        val = self.ecnt[eng]
        sem = self.esem[eng]

        def emit(h, waits=waits, fn=fn, sem=sem):
            for (s, v) in waits:
                h.wait_ge(s, v)
            fn(h).then_inc(sem, 1)
        self.ops[eng].append(emit)
        self.ninstr[eng] += 1 + len(waits)
        self._commit(eng, sem, val, reads, writes)

    def dma(self, q, out, in_, semtok, reads=(), writes=()):
        reads = [self.tok(k) for k in reads]
        writes = [self.tok(k) for k in writes]
        st = self.tok(semtok)
        if st.sem is None:
            st.sem = self.stack.enter_context(self.nc.semaphore("d%d" % self.nsem))
            self.nsem += 1
            self.dtoks.append(st)
        waits = self._deps(q, reads, writes)
        st.cnt += 16
        val = st.cnt
        sem = st.sem
        key = "D:%s" % (st.name,)

        def emit(h, waits=waits, sem=sem, out=out, in_=in_):
            for (s, v) in waits:
                h.wait_ge(s, v)
            h.dma_start(out=out, in_=in_).then_inc(sem, 16)
        self.ops[q].append(emit)
        self.ninstr[q] += 1 + len(waits)
        self._commit(key, sem, val, reads, writes)

    def barrier(self):
        targets = [(e, self.esem[e], self.ecnt[e]) for e in self.ENG]
        targets += [("D:%s" % (t.name,), t.sem, t.cnt) for t in self.dtoks]
        for eng in self.ENG:
            waits = []
            wd = self.waited[eng]
            for (k, s, v) in targets:
                if v == 0 or wd.get(k, 0) >= v:
                    continue
                wd[k] = v
                waits.append((s, v))
            if waits:
                def emit(h, waits=waits):
                    for (s, v) in waits:
                        h.wait_ge(s, v)
                self.ops[eng].append(emit)
                self.ninstr[eng] += len(waits)

    def finish(self):
        self.barrier()
        nc = self.nc
        ops = self.ops
        with nc.Block() as block:
            @block.tensor
            def _(h):
                for f in ops["pe"]:
                    f(h)

            @block.scalar
            def _(h):
                for f in ops["act"]:
                    f(h)

            @block.vector
            def _(h):
                for f in ops["dve"]:
                    f(h)

            @block.gpsimd
            def _(h):
                for f in ops["pool"]:
                    f(h)

            @block.sync
            def _(h):
                for f in ops["sp"]:
                    f(h)


class Ctx:
    pass


_uid = [0]


def _alloc(nc, st, kind, shape, dt):
    _uid[0] += 1
    name = "%s%d" % (kind, _uid[0])
    if kind == "ps":
        return st.enter_context(nc.psum_tensor(name, list(shape), dt))
    return st.enter_context(nc.sbuf_tensor(name, list(shape), dt))


def token_tiles():
    tiles = [(0, 128)]
    for i in range(8):
        tiles.append((128 + 512 * i, 512))
    return tiles


def phase_cast(K, layers):
    S = K.S
    K.cast_keys = lambda l, kind, rows: [("wb", l, kind, r0) for r0 in range(0, rows, 512)]
    for l in layers:
        for kind, src, dst, rows in (("win", K.w_in, K.w_in_b, NIN * 128), ("wout", K.w_out, K.w_out_b, 2048),
                                     ("wup", K.w_up, K.w_up_b, 88 * 128), ("wdn", K.w_down, K.w_down_b, FH)):
            s2 = src[l]
            d2 = dst[l]
            step = 512
            for r0 in range(0, rows, step):
                r1 = min(rows, r0 + step)
                S.dma("pool", d2[r0:r1, :], s2[r0:r1, :], ("cast", l, kind), reads=[], writes=[("wb", l, kind, r0)])


def phase_ln(K, li, src_fn, dst_fn, write_hT=True):
    S, nc = K.S, K.nc
    with ExitStack() as st:
        sb = lambda sh, dt: _alloc(nc, st, "sb", sh, dt)
        gbc = sb([128, D], F32)
        bbc = sb([128, D], F32)
        yb = [sb([128, D], F32) for _ in range(3)]
        xn = [sb([128, D], F32) for _ in range(2)]
        hb = [sb([128, D], F32) for _ in range(2)]
        hbf = [sb([128, D], BF16) for _ in range(2)]
        hTt = [sb([128, KC, 512], BF16) for _ in range(2)]
        stats = [sb([128, 4, 6], F32) for _ in range(2)]
        mv = [sb([128, 2], F32) for _ in range(2)]
        rstd = [sb([128, 1], F32) for _ in range(2)]
        pT = [_alloc(nc, st, "ps", [128, 512], BF16) for _ in range(4)]
        S.dma("sp", gbc[:], K.lng[li].partition_broadcast(128), "ln_g", writes=["ln_gbc"])
        S.dma("sp", bbc[:], K.lnb[li].partition_broadcast(128), "ln_b", writes=["ln_bbc"])
        blocks = []
        for ti, (t0, tw) in enumerate(token_tiles()):
            nblk = tw // 128
            for bj in range(nblk):
                blocks.append((t0 // 128 + bj, ti, bj, nblk, t0, tw))
        npt = [0]

        def stage0(blk):
            i = blk[0]
            src_fn(S, i, yb[i % 3], ("ln_y", i % 3))

        def stage1(blk):
            i, ti, bj, nblk, t0, tw = blk
            s = i % 2
            r = i % 3
            for q in range(4):
                S.op("dve", lambda h, s=s, r=r, q=q: h.bn_stats(stats[s][:, q, :], yb[r][:, q * 512:(q + 1) * 512]),
                     reads=[("ln_y", r)], writes=[("ln_st", s, q)])
            S.op("dve", lambda h, s=s: h.bn_aggr(mv[s][:], stats[s][:]),
                 reads=[("ln_st", s, q) for q in range(4)], writes=[("ln_mv", s)])
            S.op("act", lambda h, s=s: h.activation(out=rstd[s][:], in_=mv[s][:, 1:2], func=AF.Sqrt, bias=LN_EPS),
                 reads=[("ln_mv", s)], writes=[("ln_rs", s)])
            S.op("dve", lambda h, s=s: h.reciprocal(rstd[s][:], rstd[s][:]), reads=[("ln_rs", s)], writes=[("ln_rs", s)])
            S.op("dve", lambda h, s=s, r=r: h.tensor_scalar(xn[s][:], yb[r][:], mv[s][:, 0:1], rstd[s][:], ALU.subtract, ALU.mult),
                 reads=[("ln_y", r), ("ln_mv", s), ("ln_rs", s)], writes=[("ln_xn", s), ("ln_xg", s, 0), ("ln_xg", s, 1)])
            GS = 1280
            S.op("dve", lambda h, s=s: h.tensor_tensor(xn[s][:, 0:GS], xn[s][:, 0:GS], gbc[:, 0:GS], ALU.mult),
                 reads=[("ln_xn", s), "ln_gbc"], writes=[("ln_xg", s, 0)])
            S.op("pool", lambda h, s=s: h.tensor_tensor(xn[s][:, GS:D], xn[s][:, GS:D], gbc[:, GS:D], ALU.mult),
                 reads=[("ln_xn", s), "ln_gbc"], writes=[("ln_xg", s, 1)])
            S.op("pool", lambda h, s=s: h.tensor_tensor(hb[s][:], xn[s][:], bbc[:], ALU.add),
                 reads=[("ln_xn", s), ("ln_xg", s, 0), ("ln_xg", s, 1), "ln_bbc"], writes=[("ln_h", s)])

        def stage2(blk):
            i, ti, bj, nblk, t0, tw = blk
            s = i % 2
            ts = ti % 2
            dst = dst_fn(i)
            if dst is not None:
                S.dma("sp", dst, hb[s][:], ("ln_hst", s), reads=[("ln_h", s)], writes=[("h", i)])
            if not write_hT:
                return
            S.op("act", lambda h, s=s: h.copy(hbf[s][:], hb[s][:]), reads=[("ln_h", s)], writes=[("ln_hbf", s)])
            for g in range(4):
                p = npt[0] % 4
                npt[0] += 1
                for j in range(4):
                    kc = 4 * g + j
                    S.op("pe", lambda h, s=s, p=p, j=j, kc=kc: h.transpose(
                        pT[p][:, j * 128:(j + 1) * 128], hbf[s][:, kc * 128:(kc + 1) * 128], K.ident[:]),
                        reads=[("ln_hbf", s), "ident"], writes=[("ln_pT", p)])
                outap = hTt[ts][:, 4 * g:4 * g + 4, bj * 128:(bj + 1) * 128]
                inap = pT[p][:, 0:512].rearrange("p (a b) -> p a b", a=4)
                if g != 3:
                    S.op("act", lambda h, outap=outap, inap=inap: h.copy(outap, inap),
                         reads=[("ln_pT", p)], writes=[("ln_hT", ts, bj, g)])
                else:
                    S.op("dve", lambda h, outap=outap, inap=inap: h.tensor_copy(outap, inap),
                         reads=[("ln_pT", p)], writes=[("ln_hT", ts, bj, g)])
            if i == 0:
                S.op("pool", lambda h, ts=ts: h.memset(hTt[ts][:, :, 0:NPAD], 0.0),
                     reads=[], writes=[("ln_hT", ts, 0, g) for g in range(4)])
            if bj == nblk - 1:
                S.dma("sp", K.hT[:, :, t0:t0 + tw].rearrange("k p t -> p k t"), hTt[ts][:, :, 0:tw], ("ln_hTst", ts),
                      reads=[("ln_hT", ts, b2, g) for b2 in range(nblk) for g in range(4)], writes=[("hT", ti)])
        stage0(blocks[0])
        stage0(blocks[1])
        stage1(blocks[0])
        for n, blk in enumerate(blocks):
            if n + 2 < len(blocks):
                stage0(blocks[n + 2])
            if n + 1 < len(blocks):
                stage1(blocks[n + 1])
            stage2(blk)
    S.barrier()


def phase_inproj(K, l):
    S, nc = K.S, K.nc
    with ExitStack() as st:
        sb = lambda sh, dt: _alloc(nc, st, "sb", sh, dt)
        hTt = [sb([128, KC, 512], BF16) for _ in range(2)]
        w = [sb([128, KC, 128], BF16) for _ in range(4)]
        ob = [sb([128, 512], BF16) for _ in range(4)]
        of = [sb([128, 512], F32) for _ in range(4)]
        pz = [_alloc(nc, st, "ps", [128, 512], F32) for _ in range(4)]
        tiles = token_tiles()
        seq = [(ti, c) for ti in range(len(tiles)) for c in range(NIN)]
        wkeys = K.cast_keys(l, "win", NIN * 128)

        def load_w(n):
            ti, c = seq[n]
            S.dma("sp", w[n % 4][:], K.w_in_b[l][c * 128:(c + 1) * 128, :].rearrange("p (k j) -> p k j", k=KC),
                  ("ip_w", n % 4), reads=wkeys, writes=[("ip_w", n % 4)])
        for n in range(3):
            load_w(n)
        n = 0
        for ti, (t0, tw) in enumerate(tiles):
            ts = ti % 2
            S.dma("sp", hTt[ts][:, :, 0:tw], K.hT[:, :, t0:t0 + tw].rearrange("k p t -> p k t"), ("ip_h", ts),
                  reads=[("hT", ti)], writes=[("ip_h", ts)])
            for c in range(NIN):
                if n + 3 < len(seq):
                    load_w(n + 3)
                p = n % 4
                for kc in range(KC):
                    S.op("pe", lambda h, p=p, kc=kc, ts=ts, tw=tw, ws=n % 4: h.matmul(
                        pz[p][:, 0:tw], w[ws][:, kc, :], hTt[ts][:, kc, 0:tw], start=(kc == 0), stop=(kc == KC - 1)),
                        reads=[("ip_w", n % 4), ("ip_h", ts)], writes=[("ip_pz", p)])
                if c < 24:
                    dst, dram, key = ob[p], K.qkvT[c, :, t0:t0 + tw], ("qkvT", c, ti)
                else:
                    dst, dram, key = of[p], K.pf[c - 24, :, t0:t0 + tw], ("pf", c - 24, ti)
                if n % 2 == 0:
                    S.op("act", lambda h, dst=dst, p=p, tw=tw: h.copy(dst[:, 0:tw], pz[p][:, 0:tw]),
                         reads=[("ip_pz", p)], writes=[("ip_o", c < 24, p)])
                else:
                    S.op("dve", lambda h, dst=dst, p=p, tw=tw: h.tensor_copy(dst[:, 0:tw], pz[p][:, 0:tw]),
                         reads=[("ip_pz", p)], writes=[("ip_o", c < 24, p)])
                S.dma("sp", dram, dst[:, 0:tw], ("ip_o", c < 24, p), reads=[("ip_o", c < 24, p)], writes=[key])
                n += 1
    S.barrier()


def rev(v):
    pairs = [list(p) for p in v.ap]
    last = pairs[-1]
    off = v.offset + (last[1] - 1) * last[0]
    pairs[-1] = [-last[0], last[1]]
    return bass.AP(v.tensor, off, pairs)


def phase_attn(K, l):
    S, nc = K.S, K.nc
    with ExitStack() as st:
        sb = lambda sh, dt: _alloc(nc, st, "sb", sh, dt)
        qkv = [[sb([128, TP], BF16) for _ in range(3)] for _ in range(2)]
        vtm = sb([128, NB, 128], BF16)
        Qrow = [sb([128, TP + 1], F32) for _ in range(2)]
        dtb = [sb([128, 512], F32) for _ in range(4)]
        Ab = [sb([128, 512], BF16) for _ in range(6)]
        ATb = [sb([128, 4, 128], BF16) for _ in range(2)]
        osb = [sb([128, TP], F32) for _ in range(2)]
        pz = [_alloc(nc, st, "ps", [128, 512], F32) for _ in range(3)]
        pAT = [_alloc(nc, st, "ps", [128, 512], BF16) for _ in range(2)]
        po = [_alloc(nc, st, "ps", [128, 128], F32) for _ in range(2)]
        pjunk = _alloc(nc, st, "ps", [128, 512], F32)
        ntiles = len(token_tiles())
        nz = 0
        na = 0
        nblk = 0

        def load_head(hd):
            s = hd % 2
            for j in range(3):
                S.dma("sp", qkv[s][j][:], K.qkvT[8 * j + hd], ("at_qkv", s, j),
                      reads=[("qkvT", 8 * j + hd, ti) for ti in range(ntiles)], writes=[("at_qkv", s, j)])
        load_head(0)
        for hd in range(H):
            hs = hd % 2
            qT, kT, vT = qkv[hs]
            if hd + 1 < H:
                load_head(hd + 1)
            for g in range(9):
                nb_ = min(4, NB - 4 * g)
                p = na % 2
                na += 1
                for j in range(nb_):
                    blk = 4 * g + j
                    S.op("pe", lambda h, p=p, j=j, blk=blk, vT=vT: h.transpose(
                        pAT[p][:, j * 128:(j + 1) * 128], vT[:, blk * 128:(blk + 1) * 128], K.ident[:]),
                        reads=[("at_qkv", hs, 2), "ident"], writes=[("at_pAT", p)])
                S.op("act", lambda h, p=p, g=g, nb_=nb_: h.copy(
                    vtm[:, 4 * g:4 * g + nb_, :], pAT[p][:, 0:nb_ * 128].rearrange("p (a b) -> p a b", a=nb_)),
                    reads=[("at_pAT", p)], writes=[("at_vtm", g)])
            chunks = []
            for b in range(NB):
                kend = 128 * (b + 1)
                nch = (kend + 511) // 512
                for c in reversed(range(nch)):
                    chunks.append((b, c, nch, kend, nblk % 2))
                nblk += 1

            def stageA(n):
                b, c, nch, kend, qs = chunks[n]
                Q = Qrow[qs]
                c0 = 512 * c
                cw = min(512, kend - c0)
                xp = n % 3
                x = n % 4
                y = n % 6
                mf = K.m0f if b == 0 else K.mtrif
                imf = K.im0f if b == 0 else K.imtrif
                if c == nch - 1:
                    S.op("pool", lambda h, Q=Q, kend=kend: h.memset(Q[:, kend:kend + 1], 1.0), reads=[], writes=[("at_Q", qs, nch)])
                S.op("pe", lambda h, xp=xp, cw=cw, c0=c0, b=b, qT=qT, kT=kT: h.matmul(
                    pz[xp][:, 0:cw], qT[:, b * 128:(b + 1) * 128], kT[:, c0:c0 + cw], start=True, stop=True),
                    reads=[("at_qkv", hs, 0), ("at_qkv", hs, 1)], writes=[("at_pz", xp)])
                for _w in range(NWARM):
                    S.op("pe", lambda h: h.matmul(pjunk[:], K.ident[:], qT[:, 0:512], start=True, stop=True),
                         reads=["ident"], writes=["at_junk"])
                S.op("act", lambda h, x=x, xp=xp, cw=cw: h.activation(out=dtb[x][:, 0:cw], in_=pz[xp][:, 0:cw], func=AF.Sigmoid, scale=-SCALE),
                     reads=[("at_pz", xp)], writes=[("at_d", x)])
                if c == nch - 1:
                    S.op("pool", lambda h, x=x, cw=cw, mf=mf: h.tensor_tensor(
                        dtb[x][:, cw - 128:cw], dtb[x][:, cw - 128:cw], mf[:], ALU.mult),
                        reads=[("at_d", x), "masks"], writes=[("at_d", x)])
                    S.op("pool", lambda h, x=x, cw=cw, imf=imf: h.tensor_tensor(
                        dtb[x][:, cw - 128:cw], dtb[x][:, cw - 128:cw], imf[:], ALU.add),
                        reads=[("at_d", x), "masks"], writes=[("at_d", x)])
                if c == 0 and b > 0:
                    S.op("pool", lambda h, x=x: h.memset(dtb[x][:, 0:NPAD], 1.0), reads=[], writes=[("at_d", x)])
                S.op("dve", lambda h, x=x, cw=cw, c0=c0, Q=Q: h.tensor_tensor_scan(
                    out=rev(Q[:, c0:c0 + cw]), data0=rev(dtb[x][:, 0:cw]), data1=rev(K.zeros[:, 0:cw]),
                    initial=Q[:, c0 + cw:c0 + cw + 1], op0=ALU.mult, op1=ALU.add),
                    reads=[("at_d", x), ("at_Q", qs, c + 1), "zeros"], writes=[("at_Q", qs, c)])
                S.op("pool" if n % 3 != 0 else "dve", lambda h, y=y, cw=cw, c0=c0, Q=Q: h.tensor_tensor(
                    Ab[y][:, 0:cw], Q[:, c0 + 1:c0 + cw + 1], Q[:, c0:c0 + cw], ALU.subtract),
                    reads=[("at_Q", qs, c), ("at_Q", qs, c + 1)], writes=[("at_A", y)])

            def stageB(n):
                b, c, nch, kend, qs = chunks[n]
                c0 = 512 * c
                cw = min(512, kend - c0)
                y = n % 6
                w = n % 2
                z = b % 2
                nkb = cw // 128
                for j in range(nkb):
                    S.op("pe", lambda h, y=y, w=w, j=j: h.transpose(
                        pAT[w][:, j * 128:(j + 1) * 128], Ab[y][:, j * 128:(j + 1) * 128], K.ident[:]),
                        reads=[("at_A", y), "ident"], writes=[("at_pAT", w)])
                outap = ATb[w][:, 0:nkb, :]
                inap = pAT[w][:, 0:nkb * 128].rearrange("p (a b) -> p a b", a=nkb)
                S.op("act", lambda h, outap=outap, inap=inap: h.copy(outap, inap),
                     reads=[("at_pAT", w)], writes=[("at_AT", w)])
                for j in range(nkb):
                    kb = c0 // 128 + j
                    first = (c == nch - 1 and j == 0)
                    lastm = (c == 0 and j == nkb - 1)
                    S.op("pe", lambda h, w=w, j=j, kb=kb, z=z, first=first, lastm=lastm: h.matmul(
                        po[z][:], vtm[:, kb, :], ATb[w][:, j, :], start=first, stop=lastm),
                        reads=[("at_AT", w), ("at_vtm", kb // 4)], writes=[("at_po", z)])
                if c == 0:
                    S.op("act", lambda h, z=z, b=b, hs=hs: h.copy(osb[hs][:, b * 128:(b + 1) * 128], po[z][:]),
                         reads=[("at_po", z)], writes=[("at_o", hs, b)])
            LOOK = 4
            for n in range(min(LOOK, len(chunks))):
                stageA(n)
            for n in range(len(chunks)):
                if n + LOOK < len(chunks):
                    stageA(n + LOOK)
                stageB(n)
            S.dma("sp", K.oT[hd], osb[hs][:], ("at_ost", hs), reads=[("at_o", hs, b) for b in range(NB)], writes=[("oT", hd)])
    S.barrier()


def phase_lru(K, l):
    S, nc = K.S, K.nc
    N0 = NPAD
    NR = TP - NPAD
    with ExitStack() as st:
        sb = lambda sh, dt: _alloc(nc, st, "sb", sh, dt)
        xr = sb([128, 3 + TP], F32)
        yg = sb([128, TP], F32)
        xc = [sb([128, TP], F32) for _ in range(2)]
        xcb = [sb([128, TP], BF16) for _ in range(2)]
        ab = sb([128, TP], F32)
        ub = sb([128, TP], F32)
        hb = sb([128, TP], F32)
        t1 = sb([128, TP], F32)
        waf = sb([128, H, 128], F32)
        wif = sb([128, H, 128], F32)
        wab = sb([128, H, 128], BF16)
        wib = sb([128, H, 128], BF16)
        c1 = sb([128, H], F32)
        rs = [sb([128, 512], F32) for _ in range(2)]
        isb = [sb([128, 512], F32) for _ in range(2)]
        a2 = [sb([128, 512], F32) for _ in range(2)]
        msb = [sb([128, 512], F32) for _ in range(2)]
        pga = [_alloc(nc, st, "ps", [128, 512], F32) for _ in range(2)]
        pgi = [_alloc(nc, st, "ps", [128, 512], F32) for _ in range(2)]
        V = K.vec[l]
        ntiles = len(token_tiles())
        S.dma("sp", waf[:], K.lru_wa[l].rearrange("h i j -> i h j"), "lr_wa", writes=["lr_waf"])
        S.dma("sp", wif[:], K.lru_wi[l].rearrange("h i j -> i h j"), "lr_wi", writes=["lr_wif"])
        S.op("dve", lambda h: h.tensor_copy(wab[:], waf[:]), reads=["lr_waf"], writes=["lr_wab"])
        S.op("dve", lambda h: h.tensor_copy(wib[:], wif[:]), reads=["lr_wif"], writes=["lr_wib"])
        S.op("pool", lambda h: h.memset(xr[:, 0:3], 0.0), writes=["lr_xr0"])
        S.op("pool", lambda h: h.memset(hb[:, 0:N0], 0.0), reads=[], writes=["lr_h0"])
        S.op("act", lambda h: h.activation(out=c1[:], in_=V[:, V_LAM:V_LAM + 8], func=AF.Exp, scale=-1.0), reads=[("vec", l)], writes=["lr_c1"])
        S.op("act", lambda h: h.activation(out=c1[:], in_=c1[:], func=AF.Ln, bias=1.0), reads=["lr_c1"], writes=["lr_c1"])
        S.op("dve", lambda h: h.tensor_scalar(c1[:], c1[:], -8.0, None, ALU.mult), reads=["lr_c1"], writes=["lr_c1"])
        tiles = [(N0 + 512 * i, min(512, TP - (N0 + 512 * i))) for i in range((NR + 511) // 512)]
        nt = [0]

        def load_x(cb):
            S.dma("sp", xr[:, 3:3 + TP], K.pf[cb], "lr_xr", reads=[("pf", cb, ti) for ti in range(ntiles)], writes=["lr_xr"])

        def load_y(cb):
            S.dma("sp", yg[:], K.pf[8 + cb], "lr_yg", reads=[("pf", 8 + cb, ti) for ti in range(ntiles)], writes=["lr_yg"])

        def conv(cb):
            s = cb % 2
            XC = xc[s]
            cwk = [V[:, V_CW + cb * 4 + k:V_CW + cb * 4 + k + 1] for k in range(4)]
            S.op("dve", lambda h, XC=XC, cb=cb, cwk=cwk: h.tensor_scalar(XC[:, N0:TP], xr[:, 3 + N0:3 + TP], cwk[3], V[:, V_CB + cb:V_CB + cb + 1], ALU.mult, ALU.add),
                 reads=["lr_xr", "lr_xr0", ("vec", l)], writes=[("lr_xc", s)])
            for k in (2, 1, 0):
                sh = 3 - k
                S.op("dve", lambda h, XC=XC, k=k, sh=sh, cwk=cwk: h.scalar_tensor_tensor(
                    out=XC[:, N0:TP], in0=xr[:, 3 + N0 - sh:3 + TP - sh], scalar=cwk[k], in1=XC[:, N0:TP], op0=ALU.mult, op1=ALU.add),
                    reads=["lr_xr", "lr_xr0", ("lr_xc", s), ("vec", l)], writes=[("lr_xc", s)])
            S.op("act", lambda h, s=s: h.copy(xcb[s][:, N0:TP], xc[s][:, N0:TP]), reads=[("lr_xc", s)], writes=[("lr_xcb", s)])

        def gelu(cb):
            S.op("dve", lambda h: h.tensor_tensor(t1[:, N0:TP], yg[:, N0:TP], yg[:, N0:TP], ALU.mult), reads=["lr_yg"], writes=["lr_t1"])
            S.op("dve", lambda h: h.tensor_scalar(t1[:, N0:TP], t1[:, N0:TP], 0.044715, 1.0, ALU.mult, ALU.add), reads=["lr_t1"], writes=["lr_t1"])
            S.op("dve", lambda h: h.tensor_tensor(t1[:, N0:TP], t1[:, N0:TP], yg[:, N0:TP], ALU.mult), reads=["lr_t1", "lr_yg"], writes=["lr_t1"])
            S.op("act", lambda h: h.activation(out=t1[:, N0:TP], in_=t1[:, N0:TP], func=AF.Sigmoid, scale=1.5957691216057308),
                 reads=["lr_t1"], writes=["lr_t1"])
            S.op("dve", lambda h: h.tensor_tensor(t1[:, N0:TP], t1[:, N0:TP], yg[:, N0:TP], ALU.mult), reads=["lr_t1", "lr_yg"], writes=["lr_t1"])

        def run_tiles(cb):
            s = cb % 2
            for (t0, tw) in tiles:
                x = nt[0] % 2
                nt[0] += 1
                S.op("pe", lambda h, x=x, t0=t0, tw=tw, cb=cb, s=s: h.matmul(pga[x][:, 0:tw], wab[:, cb, :], xcb[s][:, t0:t0 + tw], start=True, stop=True),
                     reads=["lr_wab", ("lr_xcb", s)], writes=[("lr_pga", x)])
                S.op("pe", lambda h, x=x, t0=t0, tw=tw, cb=cb, s=s: h.matmul(pgi[x][:, 0:tw], wib[:, cb, :], xcb[s][:, t0:t0 + tw], start=True, stop=True),
                     reads=["lr_wib", ("lr_xcb", s)], writes=[("lr_pgi", x)])
                S.op("act", lambda h, x=x, tw=tw, cb=cb: h.activation(out=rs[x][:, 0:tw], in_=pga[x][:, 0:tw], func=AF.Sigmoid,
                                                                       bias=V[:, V_BA + cb:V_BA + cb + 1]),
                     reads=[("lr_pga", x), ("vec", l)], writes=[("lr_r", x)])
                S.op("act", lambda h, x=x, tw=tw, cb=cb: h.activation(out=isb[x][:, 0:tw], in_=pgi[x][:, 0:tw], func=AF.Sigmoid,
                                                                       bias=V[:, V_BI + cb:V_BI + cb + 1]),
                     reads=[("lr_pgi", x), ("vec", l)], writes=[("lr_i", x)])
                S.op("act", lambda h, x=x, t0=t0, tw=tw, cb=cb: h.activation(out=ab[:, t0:t0 + tw], in_=rs[x][:, 0:tw], func=AF.Exp,
                                                                              scale=c1[:, cb:cb + 1]),
                     reads=[("lr_r", x), "lr_c1"], writes=[("lr_a", t0)])
                S.op("pool", lambda h, x=x, t0=t0, tw=tw: h.tensor_tensor(a2[x][:, 0:tw], ab[:, t0:t0 + tw], ab[:, t0:t0 + tw], ALU.mult),
                     reads=[("lr_a", t0)], writes=[("lr_a2", x)])
                S.op("act", lambda h, x=x, tw=tw: h.activation(out=msb[x][:, 0:tw], in_=a2[x][:, 0:tw], func=AF.Sqrt, scale=-1.0, bias=1.0),
                     reads=[("lr_a2", x)], writes=[("lr_m", x)])
                S.op("pool", lambda h, x=x, t0=t0, tw=tw, s=s: h.tensor_tensor(isb[x][:, 0:tw], isb[x][:, 0:tw], xc[s][:, t0:t0 + tw], ALU.mult),
                     reads=[("lr_i", x), ("lr_xc", s)], writes=[("lr_i", x)])
                S.op("pool", lambda h, x=x, t0=t0, tw=tw: h.tensor_tensor(ub[:, t0:t0 + tw], msb[x][:, 0:tw], isb[x][:, 0:tw], ALU.mult),
                     reads=[("lr_i", x), ("lr_m", x)], writes=[("lr_u", t0)])

        def fin(cb):
            S.op("dve", lambda h: h.tensor_tensor_scan(out=hb[:, N0:TP], data0=ab[:, N0:TP], data1=ub[:, N0:TP], initial=0.0,
                                                        op0=ALU.mult, op1=ALU.add),
                 reads=[("lr_a", t0) for (t0, _) in tiles] + [("lr_u", t0) for (t0, _) in tiles], writes=["lr_h"])
            S.op("dve", lambda h: h.tensor_tensor(hb[:, N0:TP], hb[:, N0:TP], t1[:, N0:TP], ALU.mult), reads=["lr_t1", "lr_h"], writes=["lr_h"])
            S.dma("sp", K.oT[8 + cb], hb[:], "lr_hst", reads=["lr_h", "lr_h0"], writes=[("oT", 8 + cb)])
        load_x(0)
        load_y(0)
        conv(0)
        if H > 1:
            load_x(1)
        for cb in range(H):
            gelu(cb)
            if cb + 1 < H:
                load_y(cb + 1)
            run_tiles(cb)
            if cb + 1 < H:
                conv(cb + 1)
                if cb + 2 < H:
                    load_x(cb + 2)
            fin(cb)
    S.barrier()


def phase_outproj(K, l):
    S, nc = K.S, K.nc
    AX = mybir.AxisListType
    with ExitStack() as st:
        sb = lambda sh, dt: _alloc(nc, st, "sb", sh, dt)
        wo = sb([128, KC, D], BF16)
        ob = [sb([128, KC, 128], F32) for _ in range(4)]
        sq = [sb([128, KC, 128], F32) for _ in range(2)]
        s8 = [sb([128, 2, 128], F32) for _ in range(2)]
        mT = [sb([128, KC, 128], BF16) for _ in range(2)]
        hin = [sb([128, D], F32) for _ in range(3)]
        ha = [sb([128, D], F32) for _ in range(2)]
        ya = [sb([128, 512], F32) for _ in range(2)]
        yo = [sb([128, D], F32) for _ in range(2)]
        rsd = [sb([128, 4], F32) for _ in range(3)]
        pss = [_alloc(nc, st, "ps", [128, 512], F32) for _ in range(3)]
        pa = [_alloc(nc, st, "ps", [128, 512], F32) for _ in range(2)]
        pl = [_alloc(nc, st, "ps", [128, 512], F32) for _ in range(2)]
        V = K.vec[l]
        wkeys = K.cast_keys(l, "wout", 2048)
        for q in range(4):
            S.dma("sp", wo[:, 4 * q:4 * q + 4, :],
                  K.w_out_b[l].rearrange("(p a) n -> p (a n)", a=KC)[:, 4 * q * D:(4 * q + 4) * D].rearrange("p (k n) -> p k n", k=4),
                  ("op_wo", q), reads=wkeys, writes=[("op_wo", q)])
        nx = [0]

        def load_ob(i):
            r = i % 4
            S.dma("sp", ob[r][:], K.oT[:, :, i * 128:(i + 1) * 128].rearrange("k p t -> p k t"), ("op_ob", r),
                  reads=[("oT", k) for k in range(16)], writes=[("op_ob", r)])

        def load_hin(i):
            r = i % 3
            S.dma("sp", hin[r][:], K.h[i * 128:(i + 1) * 128, :], ("op_hin", r), reads=[("h", i)], writes=[("op_hin", r)])

        def prepA(i):
            s = i % 2
            r = i % 4
            u = i % 3
            S.op("act", lambda h, s=s, r=r: h.activation(out=sq[s][:], in_=ob[r][:], func=AF.Square), reads=[("op_ob", r)], writes=[("op_sq", s)])
            for g in range(2):
                S.op("dve", lambda h, s=s, g=g: h.tensor_reduce(out=s8[s][:, g, :], in_=sq[s][:, 8 * g:8 * g + 8, :].rearrange("p k t -> p t k"),
                                                               axis=AX.X, op=ALU.add),
                     reads=[("op_sq", s)], writes=[("op_s8", s, g)])
                S.op("pe", lambda h, s=s, u=u, g=g: h.matmul(pss[u][:, 2 * g:2 * g + 2], s8[s][:, g, :], K.onesf2[:], start=True, stop=True),
                     reads=[("op_s8", s, g), "ones"], writes=[("op_pss", u)])
            S.op("act", lambda h, u=u: h.activation(out=rsd[u][:], in_=pss[u][:, 0:4], func=AF.Sqrt, scale=1.0 / 1024.0, bias=RMS_EPS),
                 reads=[("op_pss", u)], writes=[("op_rs", u)])
            S.op("dve", lambda h, u=u: h.reciprocal(rsd[u][:], rsd[u][:]), reads=[("op_rs", u)], writes=[("op_rs", u)])

        def prepB(i):
            s = i % 2
            r = i % 4
            r3 = i % 3
            for kc in range(KC):
                eng = "dve" if kc % 2 == 0 else "pool"
                S.op(eng, lambda h, s=s, r=r, kc=kc: h.tensor_scalar(mT[s][:, kc, :], ob[r][:, kc, :], V[:, V_G + kc:V_G + kc + 1], 0.0, ALU.mult, ALU.add),
                     reads=[("op_ob", r), ("vec", l)], writes=[("op_mT", s, kc)])
            S.op("pool", lambda h, s=s, r3=r3: h.tensor_scalar(ha[s][:], hin[r3][:], ALPHA, 0.0, ALU.mult, ALU.add), reads=[("op_hin", r3)], writes=[("op_ha", s)])

        def main(i):
            s = i % 2
            u = i % 3
            for n in range(4):
                x = nx[0] % 2
                nx[0] += 1
                for g, pp in ((0, pa), (1, pl)):
                    for kk in range(8):
                        kc = 8 * g + kk
                        S.op("pe", lambda h, pp=pp, x=x, s=s, kc=kc, kk=kk, n=n: h.matmul(
                            pp[x][:], mT[s][:, kc, :], wo[:, kc, n * 512:(n + 1) * 512], start=(kk == 0), stop=(kk == 7)),
                            reads=[("op_mT", s, kc), ("op_wo", kc // 4)], writes=[("op_p", g, x)])
                S.op("act", lambda h, x=x, u=u: h.activation(out=ya[x][:], in_=pa[x][:], func=AF.Copy, scale=rsd[u][:, 0:1]),
                     reads=[("op_p", 0, x), ("op_rs", u)], writes=[("op_ya", x)])
                S.op("dve", lambda h, x=x, u=u: h.scalar_tensor_tensor(out=ya[x][:], in0=pl[x][:], scalar=rsd[u][:, 2:3], in1=ya[x][:],
                                                                        op0=ALU.mult, op1=ALU.add),
                     reads=[("op_p", 1, x), ("op_rs", u), ("op_ya", x)], writes=[("op_ya", x)])
                S.op("pool", lambda h, x=x, s=s, n=n: h.tensor_tensor(yo[s][:, n * 512:(n + 1) * 512], ya[x][:], ha[s][:, n * 512:(n + 1) * 512], ALU.add),
                     reads=[("op_ya", x), ("op_ha", s)], writes=[("op_yo", s, n)])
            S.dma("sp", K.ypre[i * 128:(i + 1) * 128, :], yo[s][:], ("op_yst", s), reads=[("op_yo", s, n) for n in range(4)], writes=[("ypre", i)])
        load_ob(0)
        load_ob(1)
        load_ob(2)
        load_hin(0)
        load_hin(1)
        prepA(0)
        prepA(1)
        prepB(0)
        for i in range(NB):
            if i + 3 < NB:
                load_ob(i + 3)
            if i + 2 < NB:
                load_hin(i + 2)
                prepA(i + 2)
            if i + 1 < NB:
                prepB(i + 1)
            main(i)
    S.barrier()


def phase_ffn_up(K, l):
    S, nc = K.S, K.nc
    with ExitStack() as st:
        sb = lambda sh, dt: _alloc(nc, st, "sb", sh, dt)
        hTt = [sb([128, KC, 512], BF16) for _ in range(2)]
        wg = [sb([128, KC, 128], BF16) for _ in range(3)]
        wv = [sb([128, KC, 128], BF16) for _ in range(3)]
        ug = [sb([128, 2 + 512], F32) for _ in range(2)]
        uv = [sb([128, 2 + 512], F32) for _ in range(2)]
        cg = [sb([128, 512], F32) for _ in range(2)]
        cv = [sb([128, 512], F32) for _ in range(2)]
        sg = [sb([128, 512], F32) for _ in range(2)]
        at = [sb([128, 512], BF16) for _ in range(3)]
        halo = sb([128, 88, 2], F32)
        pg = [_alloc(nc, st, "ps", [128, 512], F32) for _ in range(2)]
        pv = [_alloc(nc, st, "ps", [128, 512], F32) for _ in range(2)]
        V = K.vec[l]
        S.op("pool", lambda h: h.memset(halo[:], 0.0), writes=[("fu_halo", j) for j in range(NHC)])
        tiles = token_tiles()
        seq = [(ti, j) for ti in range(len(tiles)) for j in range(NHC)]
        wkeys = K.cast_keys(l, "wup", 88 * 128)

        def load_w(n):
            ti, j = seq[n]
            S.dma("sp", wg[n % 3][:], K.w_up_b[l][j * 128:(j + 1) * 128, :].rearrange("p (k j) -> p k j", k=KC),
                  ("fu_wg", n % 3), reads=wkeys, writes=[("fu_wg", n % 3)])
            S.dma("sp", wv[n % 3][:], K.w_up_b[l][(NHC + j) * 128:(NHC + j + 1) * 128, :].rearrange("p (k j) -> p k j", k=KC),
                  ("fu_wv", n % 3), reads=wkeys, writes=[("fu_wv", n % 3)])
        load_w(0)
        load_w(1)
        n = 0
        fcw = lambda c, k: V[:, V_FCW + c * 3 + k:V_FCW + c * 3 + k + 1]
        fcb = lambda c: V[:, V_FCB + c:V_FCB + c + 1]
        for ti, (t0, tw) in enumerate(tiles):
            ts = ti % 2
            S.dma("sp", hTt[ts][:, :, 0:tw], K.hT[:, :, t0:t0 + tw].rearrange("k p t -> p k t"), ("fu_h", ts),
                  reads=[("hT", ti)], writes=[("fu_h", ts)])
            for j in range(NHC):
                if n + 2 < len(seq):
                    load_w(n + 2)
                x = n % 2
                w3 = n % 3
                for (pp, ww, nm) in ((pg, wg, "fu_wg"), (pv, wv, "fu_wv")):
                    for kc in range(KC):
                        S.op("pe", lambda h, pp=pp, ww=ww, x=x, kc=kc, ts=ts, tw=tw, w3=w3: h.matmul(
                            pp[x][:, 0:tw], ww[w3][:, kc, :], hTt[ts][:, kc, 0:tw], start=(kc == 0), stop=(kc == KC - 1)),
                            reads=[(nm, w3), ("fu_h", ts)], writes=[(nm + "_p", x)])
                S.op("act", lambda h, x=x, tw=tw: h.copy(ug[x][:, 2:2 + tw], pg[x][:, 0:tw]), reads=[("fu_wg_p", x)], writes=[("fu_ug", x)])
                S.op("act", lambda h, x=x, tw=tw: h.copy(uv[x][:, 2:2 + tw], pv[x][:, 0:tw]), reads=[("fu_wv_p", x)], writes=[("fu_uv", x)])
                S.op("pool", lambda h, x=x, j=j: h.tensor_copy(ug[x][:, 0:2], halo[:, j, :]), reads=[("fu_halo", j)], writes=[("fu_ugh", x)])
                S.op("pool", lambda h, x=x, j=j: h.tensor_copy(uv[x][:, 0:2], halo[:, NHC + j, :]), reads=[("fu_halo", j)], writes=[("fu_uvh", x)])
                S.op("pool", lambda h, x=x, j=j, tw=tw: h.tensor_copy(halo[:, j, :], ug[x][:, tw:tw + 2]), reads=[("fu_ug", x), ("fu_ugh", x)], writes=[("fu_halo", j)])
                S.op("pool", lambda h, x=x, j=j, tw=tw: h.tensor_copy(halo[:, NHC + j, :], uv[x][:, tw:tw + 2]), reads=[("fu_uv", x), ("fu_uvh", x)], writes=[("fu_halo", j)])
                for (uu, cc, ch, nm) in ((ug, cg, j, "g"), (uv, cv, NHC + j, "v")):
                    S.op("dve", lambda h, uu=uu, cc=cc, ch=ch, x=x, tw=tw: h.tensor_scalar(
                        cc[x][:, 0:tw], uu[x][:, 2:2 + tw], fcw(ch, 2), fcb(ch), ALU.mult, ALU.add),
                        reads=[("fu_u" + nm, x), ("vec", l)], writes=[("fu_c" + nm, x)])
                    for k in (1, 0):
                        S.op("dve", lambda h, uu=uu, cc=cc, ch=ch, x=x, tw=tw, k=k: h.scalar_tensor_tensor(
                            out=cc[x][:, 0:tw], in0=uu[x][:, k:k + tw], scalar=fcw(ch, k), in1=cc[x][:, 0:tw], op0=ALU.mult, op1=ALU.add),
                            reads=[("fu_u" + nm, x), ("fu_u" + nm + "h", x), ("fu_c" + nm, x), ("vec", l)], writes=[("fu_c" + nm, x)])
                S.op("act", lambda h, x=x, tw=tw: h.activation(out=sg[x][:, 0:tw], in_=cg[x][:, 0:tw], func=AF.Silu),
                     reads=[("fu_cg", x)], writes=[("fu_sg", x)])
                S.op("pool", lambda h, x=x, w3=w3, tw=tw: h.tensor_tensor(at[w3][:, 0:tw], sg[x][:, 0:tw], cv[x][:, 0:tw], ALU.mult),
                     reads=[("fu_sg", x), ("fu_cv", x)], writes=[("fu_at", w3)])
                S.dma("sp", K.aT[j, :, t0:t0 + tw], at[w3][:, 0:tw], ("fu_at", w3), reads=[("fu_at", w3)], writes=[("aT", j, ti)])
                n += 1
    S.barrier()


def phase_ffn_down(K, l):
    S, nc = K.S, K.nc
    NW = 512
    with ExitStack() as st:
        sb = lambda sh, dt: _alloc(nc, st, "sb", sh, dt)
        at = [sb([128, NHC, 512], BF16) for _ in range(2)]
        wd = [sb([128, NHC, NW], BF16) for _ in range(2)]
        hin = [sb([128, NW], F32) for _ in range(2)]
        yo = [sb([128, NW], F32) for _ in range(2)]
        pd = [_alloc(nc, st, "ps", [128, NW], F32) for _ in range(4)]
        tiles = token_tiles()
        wkeys = K.cast_keys(l, "wdn", FH)
        nchunk = D // NW
        seq = [(ti, n) for ti in range(len(tiles)) for n in range(nchunk)]

        def load_w(q):
            ti, n = seq[q]
            S.dma("sp", wd[q % 2][:], K.w_down_b[l].rearrange("(p a) n -> p (a n)", a=NHC).rearrange("p (j n) -> p j n", j=NHC)[:, :, n * NW:(n + 1) * NW],
                  ("fd_wd", q % 2), reads=wkeys, writes=[("fd_wd", q % 2)])

        def load_at(ti):
            t0, tw = tiles[ti]
            S.dma("sp", at[ti % 2][:, :, 0:tw], K.aT[:, :, t0:t0 + tw].rearrange("j p t -> p j t"), ("fd_at", ti % 2),
                  reads=[("aT", j, ti) for j in range(NHC)], writes=[("fd_at", ti % 2)])
        load_at(0)
        load_w(0)
        q = 0
        m = 0
        for ti, (t0, tw) in enumerate(tiles):
            ts = ti % 2
            if ti + 1 < len(tiles):
                load_at(ti + 1)
            for n in range(nchunk):
                if q + 1 < len(seq):
                    load_w(q + 1)
                for bj in range(tw // 128):
                    i = t0 // 128 + bj
                    x = m % 4
                    y = m % 2
                    m += 1
                    S.dma("sp", hin[y][:], K.h[i * 128:(i + 1) * 128, n * NW:(n + 1) * NW], ("fd_hin", y), reads=[("h", i)], writes=[("fd_hin", y)])
                    for hc in range(NHC):
                        S.op("pe", lambda h, x=x, ts=ts, hc=hc, bj=bj, q=q: h.matmul(
                            pd[x][:], at[ts][:, hc, bj * 128:(bj + 1) * 128], wd[q % 2][:, hc, :], start=(hc == 0), stop=(hc == NHC - 1)),
                            reads=[("fd_at", ts), ("fd_wd", q % 2)], writes=[("fd_pd", x)])
                    S.op("dve", lambda h, x=x, y=y: h.scalar_tensor_tensor(out=yo[y][:], in0=hin[y][:], scalar=ALPHA, in1=pd[x][:],
                                                                            op0=ALU.mult, op1=ALU.add),
                         reads=[("fd_pd", x), ("fd_hin", y)], writes=[("fd_yo", y)])
                    S.dma("sp", K.ypre[i * 128:(i + 1) * 128, n * NW:(n + 1) * NW], yo[y][:], ("fd_yo", y), reads=[("fd_yo", y)], writes=[("ypre", i)])
                q += 1
    S.barrier()


def build_nc(nlayers=DEPTH, debug=False, stop=None):
    nc = bass.Bass("TRN2", target_bir_lowering=False)
    K = Ctx()
    K.nc = nc
    K.nlayers = nlayers
    dt_in = lambda name, shape: nc.dram_tensor(name, list(shape), F32, kind="ExternalInput").ap()
    kind_dbg = "ExternalOutput" if debug else "Internal"
    scr = lambda name, shape, dt: nc.dram_tensor(name, list(shape), dt, kind=kind_dbg).ap()
    K.x = dt_in("x", [SEQ, D])
    K.meta = dt_in("meta", [NMETA, D])
    K.lng = dt_in("lng", [9, D])
    K.lnb = dt_in("lnb", [9, D])
    NL = nlayers
    K.w_in = dt_in("w_in_t", [NL, NIN * 128, KC * 128])
    K.w_out = dt_in("w_out_t", [NL, 2048, 2048])
    K.w_up = dt_in("w_up_t", [NL, 88 * 128, KC * 128])
    K.w_down = dt_in("w_down_t", [NL, FH, 2048])
    K.lru_wa = dt_in("lru_wa", [NL, H, 128, 128])
    K.lru_wi = dt_in("lru_wi", [NL, H, 128, 128])
    K.vecd = dt_in("vecp", [NL, 128, NV])
    K.out = nc.dram_tensor("out", [SEQ, D], F32, kind="ExternalOutput").ap()
    wsc = lambda name, shape: nc.dram_tensor(name, list(shape), BF16, kind="Internal").ap()
    K.w_in_b = wsc("w_in_b", [NL, NIN * 128, KC * 128])
    K.w_out_b = wsc("w_out_b", [NL, 2048, 2048])
    K.w_up_b = wsc("w_up_b", [NL, 88 * 128, KC * 128])
    K.w_down_b = wsc("w_down_b", [NL, FH, 2048])
    K.h = scr("h_s", [TP, D], F32)
    K.ypre = scr("ypre_s", [TP, D], F32)
    K.hT = scr("hT_s", [KC, 128, TP], BF16)
    K.qkvT = scr("qkvT_s", [24, 128, TP], BF16)
    K.pf = scr("pf_s", [16, 128, TP], F32)
    K.oT = scr("oT_s", [16, 128, TP], F32)
    K.aT = scr("aT_s", [NHC, 128, TP], BF16)
    K.w_out = K.w_out
    with ExitStack() as st:
        S = Sched(nc, st)
        K.S = S
        sb = lambda sh, dt: _alloc(nc, st, "sb", sh, dt)
        K.ones = sb([128, 512], F32)
        K.onesf2 = sb([128, 2], F32)
        identf = sb([128, 128], F32)
        K.ident = sb([128, 128], BF16)
        K.mtrif = sb([128, 128], F32)
        K.m0f = sb([128, 128], F32)
        K.vec = [sb([128, NV], F32) for _ in range(nlayers)]
        S.op("pool", lambda h: h.memset(K.ones[:], 1.0), writes=["ones"])
        S.op("pool", lambda h: h.memset(K.onesf2[:], 1.0), writes=["ones"])
        S.op("pool", lambda h: h.memset(identf[:], 0.0), writes=["identf"])
        S.op("pool", lambda h: h.affine_select(out=identf[:], in_=K.ones[:, 0:128], pattern=[[-1, 128]], compare_op=ALU.is_equal,
                                                fill=0.0, base=0, channel_multiplier=1), reads=["ones"], writes=["identf"])
        S.op("dve", lambda h: h.tensor_copy(K.ident[:], identf[:]), reads=["identf"], writes=["ident"])
        S.op("pool", lambda h: h.affine_select(out=K.mtrif[:], in_=K.ones[:, 0:128], pattern=[[-1, 128]], compare_op=ALU.is_gt,
                                                fill=0.0, base=0, channel_multiplier=1), reads=["ones"], writes=["masks"])
        S.op("pool", lambda h: h.tensor_copy(K.m0f[:], K.mtrif[:]), reads=["masks"], writes=["masks0"])
        S.op("pool", lambda h: h.memset(K.m0f[:, 0:NPAD], 0.0), reads=["masks0"], writes=["masks0"])
        K.zeros = sb([128, 512], F32)
        K.imtrif = sb([128, 128], F32)
        K.im0f = sb([128, 128], F32)
        S.op("pool", lambda h: h.memset(K.zeros[:], 0.0), writes=["zeros"])
        S.op("dve", lambda h: h.tensor_scalar(K.imtrif[:], K.mtrif[:], -1.0, 1.0, ALU.mult, ALU.add), reads=["masks"], writes=["imasks"])
        S.op("dve", lambda h: h.tensor_scalar(K.im0f[:], K.m0f[:], -1.0, 1.0, ALU.mult, ALU.add), reads=["masks0"], writes=["imasks"])
        for l in range(nlayers):
            S.dma("sp", K.vec[l][:], K.vecd[l], ("vecld", l), writes=[("vec", l)])
        S.barrier()
        phase_cast(K, [0])

        def src0(S_, i, tile, key):
            if i == 0:
                S_.op("pool", lambda h: h.memset(tile[:], 0.0), writes=[key])
                S_.dma("sp", tile[NPAD:128, :], K.meta, ("ln_yld", key), writes=[key])
            else:
                S_.dma("sp", tile[:], K.x[(i - 1) * 128:i * 128, :], ("ln_yld", key), writes=[key])

        def srcy(S_, i, tile, key):
            S_.dma("sp", tile[:], K.ypre[i * 128:(i + 1) * 128, :], ("ln_yld", key), reads=[("ypre", i)], writes=[key])
        dsth = lambda i: K.h[i * 128:(i + 1) * 128, :]
        dstout = lambda i: (None if i == 0 else K.out[(i - 1) * 128:i * 128, :])
        phases = []
        phases.append(("ln0", lambda: phase_ln(K, 0, src0, dsth)))
        for l in range(nlayers):
            last = (l == nlayers - 1)
            phases.append(("inproj%d" % l, lambda l=l: phase_inproj(K, l)))
            phases.append(("attn%d" % l, lambda l=l: phase_attn(K, l)))
            if l == 0 and nlayers > 1:
                phases.append(("cast_rest", lambda: phase_cast(K, list(range(1, nlayers)))))
            phases.append(("lru%d" % l, lambda l=l: phase_lru(K, l)))
            phases.append(("outproj%d" % l, lambda l=l: phase_outproj(K, l)))
            phases.append(("ln1_%d" % l, lambda l=l: phase_ln(K, 1 + 2 * l, srcy, dsth)))
            phases.append(("ffnup%d" % l, lambda l=l: phase_ffn_up(K, l)))
            phases.append(("ffndn%d" % l, lambda l=l: phase_ffn_down(K, l)))
            if last and not debug:
                phases.append(("ln2_%d" % l, lambda l=l: phase_ln(K, 2 + 2 * l, srcy, dstout, write_hT=False)))
            else:
                phases.append(("ln2_%d" % l, lambda l=l: phase_ln(K, 2 + 2 * l, srcy, dsth)))
        for name, fn in phases:
            fn()
            if stop is not None and name == stop:
                break
        S.finish()
        K.ninstr = dict(S.ninstr)
        K.nsem = S.nsem
    return nc, K


def host_layout(inputs, nl=DEPTH):
    f = lambda a: np.ascontiguousarray(np.asarray(a, dtype=np.float32))
    w_in = f(inputs["w_in"])
    w_in_t = np.ascontiguousarray(w_in.reshape(DEPTH, KC, 128, NIN, 128).transpose(0, 3, 2, 1, 4)).reshape(DEPTH, NIN * 128, KC * 128)
    w_out = f(inputs["w_out"])
    w_out_t = np.ascontiguousarray(w_out.reshape(DEPTH, KC, 128, D).transpose(0, 2, 1, 3)).reshape(DEPTH, 128, KC * D).reshape(DEPTH, 2048, 2048)
    w_up = f(inputs["ffn_w_up"])
    w_up_t = np.ascontiguousarray(w_up.reshape(DEPTH, KC, 128, 88, 128).transpose(0, 3, 2, 1, 4)).reshape(DEPTH, 88 * 128, KC * 128)
    w_dn = f(inputs["ffn_w_down"])
    w_dn_t = np.ascontiguousarray(w_dn.reshape(DEPTH, NHC, 128, D).transpose(0, 2, 1, 3)).reshape(DEPTH, 128, NHC * D).reshape(DEPTH, FH, 2048)
    vec = np.zeros((DEPTH, 128, NV), np.float32)
    pc = lambda v, n: v.reshape(DEPTH, n, 128).transpose(0, 2, 1)
    vec[:, :, V_CW:V_CW + 32] = f(inputs["lru_conv_w"]).reshape(DEPTH, 4, 8, 128).transpose(0, 3, 2, 1).reshape(DEPTH, 128, 32)
    vec[:, :, V_CB:V_CB + 8] = pc(f(inputs["lru_conv_b"]), 8)
    vec[:, :, V_BA:V_BA + 8] = pc(f(inputs["lru_b_a"]), 8)
    vec[:, :, V_BI:V_BI + 8] = pc(f(inputs["lru_b_i"]), 8)
    vec[:, :, V_LAM:V_LAM + 8] = pc(f(inputs["lru_lambda"]), 8)
    gcat = np.concatenate([f(inputs["g_attn"]), f(inputs["g_lru"])], axis=1)
    vec[:, :, V_G:V_G + 16] = pc(gcat, 16)
    vec[:, :, V_FCW:V_FCW + 264] = f(inputs["ffn_conv_w"]).reshape(DEPTH, 3, 88, 128).transpose(0, 3, 2, 1).reshape(DEPTH, 128, 264)
    vec[:, :, V_FCB:V_FCB + 88] = pc(f(inputs["ffn_conv_b"]), 88)
    lng = np.zeros((9, D), np.float32)
    lnb = np.zeros((9, D), np.float32)
    lng[0] = f(inputs["ln0_g"])
    lnb[0] = f(inputs["ln0_b"])
    for l in range(DEPTH):
        lng[1 + 2 * l] = f(inputs["ln1_g"])[l]
        lnb[1 + 2 * l] = f(inputs["ln1_b"])[l]
        lng[2 + 2 * l] = f(inputs["ln2_g"])[l]
        lnb[2 + 2 * l] = f(inputs["ln2_b"])[l]
    shared = {
        "meta": f(inputs["meta_tokens"]), "lng": lng, "lnb": lnb,
        "w_in_t": w_in_t[:nl], "w_out_t": w_out_t[:nl], "w_up_t": w_up_t[:nl], "w_down_t": w_dn_t[:nl],
        "lru_wa": f(inputs["lru_w_a"])[:nl], "lru_wi": f(inputs["lru_w_i"])[:nl], "vecp": vec[:nl],
    }
    return shared


_cache = {}


def kernel(**inputs):
    x = np.asarray(inputs["x"], dtype=np.float32)
    B = x.shape[0]
    shared = host_layout(inputs)
    if "nc" not in _cache:
        _cache["nc"] = build_nc()[0]
    nc = _cache["nc"]
    in_maps = []
    for c in range(B):
        m = dict(shared)
        m["x"] = np.ascontiguousarray(x[c])
        in_maps.append(m)
    res = run_bass_kernel_spmd(nc, in_maps, core_ids=list(range(B)))
    out = np.stack([np.asarray(res.results[c]["out"], dtype=np.float32) for c in range(B)], axis=0)
    return out
```

```python
import numpy as np
from contextlib import ExitStack
import concourse.bass as bass
import concourse.mybir as mybir
from concourse.bass_utils import run_bass_kernel_spmd

F32 = mybir.dt.float32
BF16 = mybir.dt.bfloat16
AF = mybir.ActivationFunctionType
ALU = mybir.AluOpType

D = 2048
SEQ = 4096
NMETA = 16
DEPTH = 4
TP = 4224
NPAD = 112
NB = 33
KC = 16
H = 8
NIN = 40
FH = 5632
NHC = 44
ALPHA = float((2 * DEPTH) ** 0.25)
SCALE = float(128 ** -0.5)
LN_EPS = 1e-5
RMS_EPS = 1e-6
NV = 432
NWARM = 0
V_CW, V_CB, V_BA, V_BI, V_LAM, V_G, V_FCW, V_FCB = 0, 32, 40, 48, 56, 64, 80, 344


class Tok:
    __slots__ = ("w", "r", "sem", "cnt", "name")

    def __init__(self, name):
        self.w = {}
        self.r = {}
        self.sem = None
        self.cnt = 0
        self.name = name


class Sched:
    ENG = ("pe", "act", "dve", "pool", "sp")

    def __init__(self, nc, stack):
        self.nc = nc
        self.stack = stack
        self.ops = {e: [] for e in self.ENG}
        self.esem = {}
        self.ecnt = {e: 0 for e in self.ENG}
        self.waited = {e: {} for e in self.ENG}
        for e in self.ENG:
            self.esem[e] = stack.enter_context(nc.semaphore("s_" + e))
        self.nsem = 5
        self.toks = {}
        self.dtoks = []
        self.ninstr = {e: 0 for e in self.ENG}

    def tok(self, key):
        if isinstance(key, Tok):
            return key
        t = self.toks.get(key)
        if t is None:
            t = Tok(key)
            self.toks[key] = t
        return t

    def _deps(self, eng, reads, writes):
        deps = {}

        def upd(d):
            for k, sv in d.items():
                o = deps.get(k)
                if o is None or o[1] < sv[1]:
                    deps[k] = sv
        for t in reads:
            upd(t.w)
        for t in writes:
            upd(t.w)
            upd(t.r)
        out = []
        wd = self.waited[eng]
        for k, (s, v) in deps.items():
            if eng == "pe" and k == "pe":
                continue
            if wd.get(k, 0) >= v:
                continue
            wd[k] = v
            out.append((s, v))
        return out

    def _commit(self, key, sem, val, reads, writes):
        for t in reads:
            o = t.r.get(key)
            if o is None or o[1] < val:
                t.r[key] = (sem, val)
        for t in writes:
            t.w = {key: (sem, val)}
            t.r = {}

    def op(self, eng, fn, reads=(), writes=()):
        reads = [self.tok(k) for k in reads]
        writes = [self.tok(k) for k in writes]
        waits = self._deps(eng, reads, writes)
        self.ecnt[eng] += 1
        val = self.ecnt[eng]
        sem = self.esem[eng]

        def emit(h, waits=waits, fn=fn, sem=sem):
            for (s, v) in waits:
                h.wait_ge(s, v)
            fn(h).then_inc(sem, 1)
        self.ops[eng].append(emit)
        self.ninstr[eng] += 1 + len(waits)
        self._commit(eng, sem, val, reads, writes)

    def dma(self, q, out, in_, semtok, reads=(), writes=()):
        reads = [self.tok(k) for k in reads]
        writes = [self.tok(k) for k in writes]
        st = self.tok(semtok)
        if st.sem is None:
            st.sem = self.stack.enter_context(self.nc.semaphore("d%d" % self.nsem))
            self.nsem += 1
            self.dtoks.append(st)
        waits = self._deps(q, reads, writes)
        st.cnt += 16
        val = st.cnt
        sem = st.sem
        key = "D:%s" % (st.name,)

        def emit(h, waits=waits, sem=sem, out=out, in_=in_):
            for (s, v) in waits:
                h.wait_ge(s, v)
            h.dma_start(out=out, in_=in_).then_inc(sem, 16)
        self.ops[q].append(emit)
        self.ninstr[q] += 1 + len(waits)
        self._commit(key, sem, val, reads, writes)

    def barrier(self):
        targets = [(e, self.esem[e], self.ecnt[e]) for e in self.ENG]
        targets += [("D:%s" % (t.name,), t.sem, t.cnt) for t in self.dtoks]
        for eng in self.ENG:
            waits = []
            wd = self.waited[eng]
            for (k, s, v) in targets:
                if v == 0 or wd.get(k, 0) >= v:
                    continue
                wd[k] = v
                waits.append((s, v))
            if waits:
                def emit(h, waits=waits):
                    for (s, v) in waits:
                        h.wait_ge(s, v)
                self.ops[eng].append(emit)
                self.ninstr[eng] += len(waits)

    def finish(self):
        self.barrier()
        nc = self.nc
        ops = self.ops
        with nc.Block() as block:
            @block.tensor
            def _(h):
                for f in ops["pe"]:
                    f(h)

            @block.scalar
            def _(h):
                for f in ops["act"]:
                    f(h)

            @block.vector
            def _(h):
                for f in ops["dve"]:
                    f(h)

            @block.gpsimd
            def _(h):
                for f in ops["pool"]:
                    f(h)

            @block.sync
            def _(h):
                for f in ops["sp"]:
                    f(h)


class Ctx:
    pass


_uid = [0]


def _alloc(nc, st, kind, shape, dt):
    _uid[0] += 1
    name = "%s%d" % (kind, _uid[0])
    if kind == "ps":
        return st.enter_context(nc.psum_tensor(name, list(shape), dt))
    return st.enter_context(nc.sbuf_tensor(name, list(shape), dt))


def token_tiles():
    tiles = [(0, 128)]
    for i in range(8):
        tiles.append((128 + 512 * i, 512))
    return tiles


def phase_cast(K, layers):
    S = K.S
    K.cast_keys = lambda l, kind, rows: [("wb", l, kind, r0) for r0 in range(0, rows, 512)]
    for l in layers:
        for kind, src, dst, rows in (("win", K.w_in, K.w_in_b, NIN * 128), ("wout", K.w_out, K.w_out_b, 2048),
                                     ("wup", K.w_up, K.w_up_b, 88 * 128), ("wdn", K.w_down, K.w_down_b, FH)):
            s2 = src[l]
            d2 = dst[l]
            step = 512
            for r0 in range(0, rows, step):
                r1 = min(rows, r0 + step)
                S.dma("pool", d2[r0:r1, :], s2[r0:r1, :], ("cast", l, kind), reads=[], writes=[("wb", l, kind, r0)])


def phase_ln(K, li, src_fn, dst_fn, write_hT=True):
    S, nc = K.S, K.nc
    with ExitStack() as st:
        sb = lambda sh, dt: _alloc(nc, st, "sb", sh, dt)
        gbc = sb([128, D], F32)
        bbc = sb([128, D], F32)
        yb = [sb([128, D], F32) for _ in range(3)]
        xn = [sb([128, D], F32) for _ in range(2)]
        hb = [sb([128, D], F32) for _ in range(2)]
        hbf = [sb([128, D], BF16) for _ in range(2)]
        hTt = [sb([128, KC, 512], BF16) for _ in range(2)]
        stats = [sb([128, 4, 6], F32) for _ in range(2)]
        mv = [sb([128, 2], F32) for _ in range(2)]
        rstd = [sb([128, 1], F32) for _ in range(2)]
        pT = [_alloc(nc, st, "ps", [128, 512], BF16) for _ in range(4)]
        S.dma("sp", gbc[:], K.lng[li].partition_broadcast(128), "ln_g", writes=["ln_gbc"])
        S.dma("sp", bbc[:], K.lnb[li].partition_broadcast(128), "ln_b", writes=["ln_bbc"])
        blocks = []
        for ti, (t0, tw) in enumerate(token_tiles()):
            nblk = tw // 128
            for bj in range(nblk):
                blocks.append((t0 // 128 + bj, ti, bj, nblk, t0, tw))
        npt = [0]

        def stage0(blk):
            i = blk[0]
            src_fn(S, i, yb[i % 3], ("ln_y", i % 3))

        def stage1(blk):
            i, ti, bj, nblk, t0, tw = blk
            s = i % 2
            r = i % 3
            for q in range(4):
                S.op("dve", lambda h, s=s, r=r, q=q: h.bn_stats(stats[s][:, q, :], yb[r][:, q * 512:(q + 1) * 512]),
                     reads=[("ln_y", r)], writes=[("ln_st", s, q)])
            S.op("dve", lambda h, s=s: h.bn_aggr(mv[s][:], stats[s][:]),
                 reads=[("ln_st", s, q) for q in range(4)], writes=[("ln_mv", s)])
            S.op("act", lambda h, s=s: h.activation(out=rstd[s][:], in_=mv[s][:, 1:2], func=AF.Sqrt, bias=LN_EPS),
                 reads=[("ln_mv", s)], writes=[("ln_rs", s)])
            S.op("dve", lambda h, s=s: h.reciprocal(rstd[s][:], rstd[s][:]), reads=[("ln_rs", s)], writes=[("ln_rs", s)])
            S.op("dve", lambda h, s=s, r=r: h.tensor_scalar(xn[s][:], yb[r][:], mv[s][:, 0:1], rstd[s][:], ALU.subtract, ALU.mult),
                 reads=[("ln_y", r), ("ln_mv", s), ("ln_rs", s)], writes=[("ln_xn", s), ("ln_xg", s, 0), ("ln_xg", s, 1)])
            S.op("dve", lambda h, s=s: h.tensor_tensor(xn[s][:], xn[s][:], gbc[:], ALU.mult),
                 reads=[("ln_xn", s), "ln_gbc"], writes=[("ln_xg", s, 0), ("ln_xg", s, 1)])
            S.op("pool", lambda h, s=s: h.tensor_tensor(hb[s][:], xn[s][:], bbc[:], ALU.add),
                 reads=[("ln_xn", s), ("ln_xg", s, 0), ("ln_xg", s, 1), "ln_bbc"], writes=[("ln_h", s)])

        def stage2(blk):
            i, ti, bj, nblk, t0, tw = blk
            s = i % 2
            ts = ti % 2
            dst = dst_fn(i)
            if dst is not None:
                S.dma("sp", dst, hb[s][:], ("ln_hst", s), reads=[("ln_h", s)], writes=[("h", i)])
            if not write_hT:
                return
            S.op("act", lambda h, s=s: h.copy(hbf[s][:], hb[s][:]), reads=[("ln_h", s)], writes=[("ln_hbf", s)])
            for g in range(4):
                p = npt[0] % 4
                npt[0] += 1
                for j in range(4):
                    kc = 4 * g + j
                    S.op("pe", lambda h, s=s, p=p, j=j, kc=kc: h.transpose(
                        pT[p][:, j * 128:(j + 1) * 128], hbf[s][:, kc * 128:(kc + 1) * 128], K.ident[:]),
                        reads=[("ln_hbf", s), "ident"], writes=[("ln_pT", p)])
                outap = hTt[ts][:, 4 * g:4 * g + 4, bj * 128:(bj + 1) * 128]
                inap = pT[p][:, 0:512].rearrange("p (a b) -> p a b", a=4)
                if g != 3:
                    S.op("act", lambda h, outap=outap, inap=inap: h.copy(outap, inap),
                         reads=[("ln_pT", p)], writes=[("ln_hT", ts, bj, g)])
                else:
                    S.op("dve", lambda h, outap=outap, inap=inap: h.tensor_copy(outap, inap),
                         reads=[("ln_pT", p)], writes=[("ln_hT", ts, bj, g)])
            if i == 0:
                S.op("pool", lambda h, ts=ts: h.memset(hTt[ts][:, :, 0:NPAD], 0.0),
                     reads=[], writes=[("ln_hT", ts, 0, g) for g in range(4)])
            if bj == nblk - 1:
                S.dma("sp", K.hT[:, :, t0:t0 + tw].rearrange("k p t -> p k t"), hTt[ts][:, :, 0:tw], ("ln_hTst", ts),
                      reads=[("ln_hT", ts, b2, g) for b2 in range(nblk) for g in range(4)], writes=[("hT", ti)])
        stage0(blocks[0])
        stage0(blocks[1])
        stage1(blocks[0])
        for n, blk in enumerate(blocks):
            if n + 2 < len(blocks):
                stage0(blocks[n + 2])
            if n + 1 < len(blocks):
                stage1(blocks[n + 1])
            stage2(blk)
    S.barrier()


def phase_inproj(K, l):
    S, nc = K.S, K.nc
    with ExitStack() as st:
        sb = lambda sh, dt: _alloc(nc, st, "sb", sh, dt)
        hTt = [sb([128, KC, 512], BF16) for _ in range(2)]
        w = [sb([128, KC, 128], BF16) for _ in range(4)]
        ob = [sb([128, 512], BF16) for _ in range(4)]
        of = [sb([128, 512], F32) for _ in range(4)]
        pz = [_alloc(nc, st, "ps", [128, 512], F32) for _ in range(4)]
        tiles = token_tiles()
        seq = [(ti, c) for ti in range(len(tiles)) for c in range(NIN)]
        wkeys = K.cast_keys(l, "win", NIN * 128)

        def load_w(n):
            ti, c = seq[n]
            S.dma("sp", w[n % 4][:], K.w_in_b[l][c * 128:(c + 1) * 128, :].rearrange("p (k j) -> p k j", k=KC),
                  ("ip_w", n % 4), reads=wkeys, writes=[("ip_w", n % 4)])
        for n in range(3):
            load_w(n)
        n = 0
        for ti, (t0, tw) in enumerate(tiles):
            ts = ti % 2
            S.dma("sp", hTt[ts][:, :, 0:tw], K.hT[:, :, t0:t0 + tw].rearrange("k p t -> p k t"), ("ip_h", ts),
                  reads=[("hT", ti)], writes=[("ip_h", ts)])
            for c in range(NIN):
                if n + 3 < len(seq):
                    load_w(n + 3)
                p = n % 4
                for kc in range(KC):
                    S.op("pe", lambda h, p=p, kc=kc, ts=ts, tw=tw, ws=n % 4: h.matmul(
                        pz[p][:, 0:tw], w[ws][:, kc, :], hTt[ts][:, kc, 0:tw], start=(kc == 0), stop=(kc == KC - 1)),
                        reads=[("ip_w", n % 4), ("ip_h", ts)], writes=[("ip_pz", p)])
                if c < 24:
                    dst, dram, key = ob[p], K.qkvT[c, :, t0:t0 + tw], ("qkvT", c, ti)
                else:
                    dst, dram, key = of[p], K.pf[c - 24, :, t0:t0 + tw], ("pf", c - 24, ti)
                if n % 2 == 0:
                    S.op("act", lambda h, dst=dst, p=p, tw=tw: h.copy(dst[:, 0:tw], pz[p][:, 0:tw]),
                         reads=[("ip_pz", p)], writes=[("ip_o", c < 24, p)])
                else:
                    S.op("dve", lambda h, dst=dst, p=p, tw=tw: h.tensor_copy(dst[:, 0:tw], pz[p][:, 0:tw]),
                         reads=[("ip_pz", p)], writes=[("ip_o", c < 24, p)])
                S.dma("sp", dram, dst[:, 0:tw], ("ip_o", c < 24, p), reads=[("ip_o", c < 24, p)], writes=[key])
                n += 1
    S.barrier()


def rev(v):
    pairs = [list(p) for p in v.ap]
    last = pairs[-1]
    off = v.offset + (last[1] - 1) * last[0]
    pairs[-1] = [-last[0], last[1]]
    return bass.AP(v.tensor, off, pairs)


def phase_attn(K, l):
    S, nc = K.S, K.nc
    with ExitStack() as st:
        sb = lambda sh, dt: _alloc(nc, st, "sb", sh, dt)
        qkv = [[sb([128, TP], BF16) for _ in range(3)] for _ in range(2)]
        vtm = sb([128, NB, 128], BF16)
        Qrow = [sb([128, TP + 1], F32) for _ in range(2)]
        dtb = [sb([128, 512], F32) for _ in range(4)]
        Ab = [sb([128, 512], BF16) for _ in range(6)]
        ATb = [sb([128, 4, 128], BF16) for _ in range(2)]
        osb = [sb([128, TP], F32) for _ in range(2)]
        pz = [_alloc(nc, st, "ps", [128, 512], F32) for _ in range(3)]
        pAT = [_alloc(nc, st, "ps", [128, 512], BF16) for _ in range(2)]
        po = [_alloc(nc, st, "ps", [128, 128], F32) for _ in range(2)]
        pjunk = _alloc(nc, st, "ps", [128, 512], F32)
        ntiles = len(token_tiles())
        nz = 0
        na = 0
        nblk = 0

        def load_head(hd):
            s = hd % 2
            for j in range(3):
                S.dma("sp", qkv[s][j][:], K.qkvT[8 * j + hd], ("at_qkv", s, j),
                      reads=[("qkvT", 8 * j + hd, ti) for ti in range(ntiles)], writes=[("at_qkv", s, j)])
        load_head(0)
        for hd in range(H):
            hs = hd % 2
            qT, kT, vT = qkv[hs]
            if hd + 1 < H:
                load_head(hd + 1)
            for g in range(9):
                nb_ = min(4, NB - 4 * g)
                p = na % 2
                na += 1
                for j in range(nb_):
                    blk = 4 * g + j
                    S.op("pe", lambda h, p=p, j=j, blk=blk, vT=vT: h.transpose(
                        pAT[p][:, j * 128:(j + 1) * 128], vT[:, blk * 128:(blk + 1) * 128], K.ident[:]),
                        reads=[("at_qkv", hs, 2), "ident"], writes=[("at_pAT", p)])
                S.op("act", lambda h, p=p, g=g, nb_=nb_: h.copy(
                    vtm[:, 4 * g:4 * g + nb_, :], pAT[p][:, 0:nb_ * 128].rearrange("p (a b) -> p a b", a=nb_)),
                    reads=[("at_pAT", p)], writes=[("at_vtm", g)])
            chunks = []
            for b in range(NB):
                kend = 128 * (b + 1)
                nch = (kend + 511) // 512
                for c in reversed(range(nch)):
                    chunks.append((b, c, nch, kend, nblk % 2))
                nblk += 1

            def stageA(n):
                b, c, nch, kend, qs = chunks[n]
                Q = Qrow[qs]
                c0 = 512 * c
                cw = min(512, kend - c0)
                xp = n % 3
                x = n % 4
                y = n % 6
                mf = K.m0f if b == 0 else K.mtrif
                imf = K.im0f if b == 0 else K.imtrif
                if c == nch - 1:
                    S.op("pool", lambda h, Q=Q, kend=kend: h.memset(Q[:, kend:kend + 1], 1.0), reads=[], writes=[("at_Q", qs, nch)])
                S.op("pe", lambda h, xp=xp, cw=cw, c0=c0, b=b, qT=qT, kT=kT: h.matmul(
                    pz[xp][:, 0:cw], qT[:, b * 128:(b + 1) * 128], kT[:, c0:c0 + cw], start=True, stop=True),
                    reads=[("at_qkv", hs, 0), ("at_qkv", hs, 1)], writes=[("at_pz", xp)])
                for _w in range(NWARM):
                    S.op("pe", lambda h: h.matmul(pjunk[:], K.ident[:], qT[:, 0:512], start=True, stop=True),
                         reads=["ident"], writes=["at_junk"])
                S.op("act", lambda h, x=x, xp=xp, cw=cw: h.activation(out=dtb[x][:, 0:cw], in_=pz[xp][:, 0:cw], func=AF.Sigmoid, scale=-SCALE),
                     reads=[("at_pz", xp)], writes=[("at_d", x)])
                if c == nch - 1:
                    S.op("pool", lambda h, x=x, cw=cw, mf=mf: h.tensor_tensor(
                        dtb[x][:, cw - 128:cw], dtb[x][:, cw - 128:cw], mf[:], ALU.mult),
                        reads=[("at_d", x), "masks"], writes=[("at_d", x)])
                    S.op("pool", lambda h, x=x, cw=cw, imf=imf: h.tensor_tensor(
                        dtb[x][:, cw - 128:cw], dtb[x][:, cw - 128:cw], imf[:], ALU.add),
                        reads=[("at_d", x), "masks"], writes=[("at_d", x)])
                if c == 0 and b > 0:
                    S.op("pool", lambda h, x=x: h.memset(dtb[x][:, 0:NPAD], 1.0), reads=[], writes=[("at_d", x)])
                S.op("dve", lambda h, x=x, cw=cw, c0=c0, Q=Q: h.tensor_tensor_scan(
                    out=rev(Q[:, c0:c0 + cw]), data0=rev(dtb[x][:, 0:cw]), data1=rev(K.zeros[:, 0:cw]),
                    initial=Q[:, c0 + cw:c0 + cw + 1], op0=ALU.mult, op1=ALU.add),
                    reads=[("at_d", x), ("at_Q", qs, c + 1), "zeros"], writes=[("at_Q", qs, c)])
                S.op("pool" if n % 3 != 0 else "dve", lambda h, y=y, cw=cw, c0=c0, Q=Q: h.tensor_tensor(
                    Ab[y][:, 0:cw], Q[:, c0 + 1:c0 + cw + 1], Q[:, c0:c0 + cw], ALU.subtract),
                    reads=[("at_Q", qs, c), ("at_Q", qs, c + 1)], writes=[("at_A", y)])

            def stageB(n):
                b, c, nch, kend, qs = chunks[n]
                c0 = 512 * c
                cw = min(512, kend - c0)
                y = n % 6
                w = n % 2
                z = b % 2
                nkb = cw // 128
                for j in range(nkb):
                    S.op("pe", lambda h, y=y, w=w, j=j: h.transpose(
                        pAT[w][:, j * 128:(j + 1) * 128], Ab[y][:, j * 128:(j + 1) * 128], K.ident[:]),
                        reads=[("at_A", y), "ident"], writes=[("at_pAT", w)])
                outap = ATb[w][:, 0:nkb, :]
                inap = pAT[w][:, 0:nkb * 128].rearrange("p (a b) -> p a b", a=nkb)
                S.op("act", lambda h, outap=outap, inap=inap: h.copy(outap, inap),
                     reads=[("at_pAT", w)], writes=[("at_AT", w)])
                for j in range(nkb):
                    kb = c0 // 128 + j
                    first = (c == nch - 1 and j == 0)
                    lastm = (c == 0 and j == nkb - 1)
                    S.op("pe", lambda h, w=w, j=j, kb=kb, z=z, first=first, lastm=lastm: h.matmul(
                        po[z][:], vtm[:, kb, :], ATb[w][:, j, :], start=first, stop=lastm),
                        reads=[("at_AT", w), ("at_vtm", kb // 4)], writes=[("at_po", z)])
                if c == 0:
                    S.op("act", lambda h, z=z, b=b, hs=hs: h.copy(osb[hs][:, b * 128:(b + 1) * 128], po[z][:]),
                         reads=[("at_po", z)], writes=[("at_o", hs, b)])
            LOOK = 4
            for n in range(min(LOOK, len(chunks))):
                stageA(n)
            for n in range(len(chunks)):
                if n + LOOK < len(chunks):
                    stageA(n + LOOK)
                stageB(n)
            S.dma("sp", K.oT[hd], osb[hs][:], ("at_ost", hs), reads=[("at_o", hs, b) for b in range(NB)], writes=[("oT", hd)])
    S.barrier()


def phase_lru(K, l):
    S, nc = K.S, K.nc
    N0 = NPAD
    NR = TP - NPAD
    with ExitStack() as st:
        sb = lambda sh, dt: _alloc(nc, st, "sb", sh, dt)
        xr = sb([128, 3 + TP], F32)
        yg = sb([128, TP], F32)
        xc = [sb([128, TP], F32) for _ in range(2)]
        xcb = [sb([128, TP], BF16) for _ in range(2)]
        ab = sb([128, TP], F32)
        ub = sb([128, TP], F32)
        hb = sb([128, TP], F32)
        t1 = sb([128, TP], F32)
        waf = sb([128, H, 128], F32)
        wif = sb([128, H, 128], F32)
        wab = sb([128, H, 128], BF16)
        wib = sb([128, H, 128], BF16)
        c1 = sb([128, H], F32)
        rs = [sb([128, 512], F32) for _ in range(2)]
        isb = [sb([128, 512], F32) for _ in range(2)]
        a2 = [sb([128, 512], F32) for _ in range(2)]
        msb = [sb([128, 512], F32) for _ in range(2)]
        pga = [_alloc(nc, st, "ps", [128, 512], F32) for _ in range(2)]
        pgi = [_alloc(nc, st, "ps", [128, 512], F32) for _ in range(2)]
        V = K.vec[l]
        ntiles = len(token_tiles())
        S.dma("sp", waf[:], K.lru_wa[l].rearrange("h i j -> i h j"), "lr_wa", writes=["lr_waf"])
        S.dma("sp", wif[:], K.lru_wi[l].rearrange("h i j -> i h j"), "lr_wi", writes=["lr_wif"])
        S.op("dve", lambda h: h.tensor_copy(wab[:], waf[:]), reads=["lr_waf"], writes=["lr_wab"])
        S.op("dve", lambda h: h.tensor_copy(wib[:], wif[:]), reads=["lr_wif"], writes=["lr_wib"])
        S.op("pool", lambda h: h.memset(xr[:, 0:3], 0.0), writes=["lr_xr0"])
        S.op("pool", lambda h: h.memset(hb[:, 0:N0], 0.0), reads=[], writes=["lr_h0"])
        S.op("act", lambda h: h.activation(out=c1[:], in_=V[:, V_LAM:V_LAM + 8], func=AF.Exp, scale=-1.0), reads=[("vec", l)], writes=["lr_c1"])
        S.op("act", lambda h: h.activation(out=c1[:], in_=c1[:], func=AF.Ln, bias=1.0), reads=["lr_c1"], writes=["lr_c1"])
        S.op("dve", lambda h: h.tensor_scalar(c1[:], c1[:], -8.0, None, ALU.mult), reads=["lr_c1"], writes=["lr_c1"])
        tiles = [(N0 + 512 * i, min(512, TP - (N0 + 512 * i))) for i in range((NR + 511) // 512)]
        nt = [0]

        def load_x(cb):
            S.dma("sp", xr[:, 3:3 + TP], K.pf[cb], "lr_xr", reads=[("pf", cb, ti) for ti in range(ntiles)], writes=["lr_xr"])

        def load_y(cb):
            S.dma("sp", yg[:], K.pf[8 + cb], "lr_yg", reads=[("pf", 8 + cb, ti) for ti in range(ntiles)], writes=["lr_yg"])

        def conv(cb):
            s = cb % 2
            XC = xc[s]
            cwk = [V[:, V_CW + cb * 4 + k:V_CW + cb * 4 + k + 1] for k in range(4)]
            S.op("dve", lambda h, XC=XC, cb=cb, cwk=cwk: h.tensor_scalar(XC[:, N0:TP], xr[:, 3 + N0:3 + TP], cwk[3], V[:, V_CB + cb:V_CB + cb + 1], ALU.mult, ALU.add),
                 reads=["lr_xr", "lr_xr0", ("vec", l)], writes=[("lr_xc", s)])
            for k in (2, 1, 0):
                sh = 3 - k
                S.op("dve", lambda h, XC=XC, k=k, sh=sh, cwk=cwk: h.scalar_tensor_tensor(
                    out=XC[:, N0:TP], in0=xr[:, 3 + N0 - sh:3 + TP - sh], scalar=cwk[k], in1=XC[:, N0:TP], op0=ALU.mult, op1=ALU.add),
                    reads=["lr_xr", "lr_xr0", ("lr_xc", s), ("vec", l)], writes=[("lr_xc", s)])
            S.op("act", lambda h, s=s: h.copy(xcb[s][:, N0:TP], xc[s][:, N0:TP]), reads=[("lr_xc", s)], writes=[("lr_xcb", s)])

        def gelu(cb):
            S.op("dve", lambda h: h.tensor_tensor(t1[:, N0:TP], yg[:, N0:TP], yg[:, N0:TP], ALU.mult), reads=["lr_yg"], writes=["lr_t1"])
            S.op("dve", lambda h: h.tensor_scalar(t1[:, N0:TP], t1[:, N0:TP], 0.044715, 1.0, ALU.mult, ALU.add), reads=["lr_t1"], writes=["lr_t1"])
            S.op("dve", lambda h: h.tensor_tensor(t1[:, N0:TP], t1[:, N0:TP], yg[:, N0:TP], ALU.mult), reads=["lr_t1", "lr_yg"], writes=["lr_t1"])
            S.op("act", lambda h: h.activation(out=t1[:, N0:TP], in_=t1[:, N0:TP], func=AF.Sigmoid, scale=1.5957691216057308),
                 reads=["lr_t1"], writes=["lr_t1"])
            S.op("dve", lambda h: h.tensor_tensor(t1[:, N0:TP], t1[:, N0:TP], yg[:, N0:TP], ALU.mult), reads=["lr_t1", "lr_yg"], writes=["lr_t1"])

        def run_tiles(cb):
            s = cb % 2
            for (t0, tw) in tiles:
                x = nt[0] % 2
                nt[0] += 1
                S.op("pe", lambda h, x=x, t0=t0, tw=tw, cb=cb, s=s: h.matmul(pga[x][:, 0:tw], wab[:, cb, :], xcb[s][:, t0:t0 + tw], start=True, stop=True),
                     reads=["lr_wab", ("lr_xcb", s)], writes=[("lr_pga", x)])
                S.op("pe", lambda h, x=x, t0=t0, tw=tw, cb=cb, s=s: h.matmul(pgi[x][:, 0:tw], wib[:, cb, :], xcb[s][:, t0:t0 + tw], start=True, stop=True),
                     reads=["lr_wib", ("lr_xcb", s)], writes=[("lr_pgi", x)])
                S.op("act", lambda h, x=x, tw=tw, cb=cb: h.activation(out=rs[x][:, 0:tw], in_=pga[x][:, 0:tw], func=AF.Sigmoid,
                                                                       bias=V[:, V_BA + cb:V_BA + cb + 1]),
                     reads=[("lr_pga", x), ("vec", l)], writes=[("lr_r", x)])
                S.op("act", lambda h, x=x, tw=tw, cb=cb: h.activation(out=isb[x][:, 0:tw], in_=pgi[x][:, 0:tw], func=AF.Sigmoid,
                                                                       bias=V[:, V_BI + cb:V_BI + cb + 1]),
                     reads=[("lr_pgi", x), ("vec", l)], writes=[("lr_i", x)])
                S.op("act", lambda h, x=x, t0=t0, tw=tw, cb=cb: h.activation(out=ab[:, t0:t0 + tw], in_=rs[x][:, 0:tw], func=AF.Exp,
                                                                              scale=c1[:, cb:cb + 1]),
                     reads=[("lr_r", x), "lr_c1"], writes=[("lr_a", t0)])
                S.op("pool", lambda h, x=x, t0=t0, tw=tw: h.tensor_tensor(a2[x][:, 0:tw], ab[:, t0:t0 + tw], ab[:, t0:t0 + tw], ALU.mult),
                     reads=[("lr_a", t0)], writes=[("lr_a2", x)])
                S.op("act", lambda h, x=x, tw=tw: h.activation(out=msb[x][:, 0:tw], in_=a2[x][:, 0:tw], func=AF.Sqrt, scale=-1.0, bias=1.0),
                     reads=[("lr_a2", x)], writes=[("lr_m", x)])
                S.op("pool", lambda h, x=x, t0=t0, tw=tw, s=s: h.tensor_tensor(isb[x][:, 0:tw], isb[x][:, 0:tw], xc[s][:, t0:t0 + tw], ALU.mult),
                     reads=[("lr_i", x), ("lr_xc", s)], writes=[("lr_i", x)])
                S.op("pool", lambda h, x=x, t0=t0, tw=tw: h.tensor_tensor(ub[:, t0:t0 + tw], msb[x][:, 0:tw], isb[x][:, 0:tw], ALU.mult),
                     reads=[("lr_i", x), ("lr_m", x)], writes=[("lr_u", t0)])

        def fin(cb):
            S.op("dve", lambda h: h.tensor_tensor_scan(out=hb[:, N0:TP], data0=ab[:, N0:TP], data1=ub[:, N0:TP], initial=0.0,
                                                        op0=ALU.mult, op1=ALU.add),
                 reads=[("lr_a", t0) for (t0, _) in tiles] + [("lr_u", t0) for (t0, _) in tiles], writes=["lr_h"])
            S.op("dve", lambda h: h.tensor_tensor(hb[:, N0:TP], hb[:, N0:TP], t1[:, N0:TP], ALU.mult), reads=["lr_t1", "lr_h"], writes=["lr_h"])
            S.dma("sp", K.oT[8 + cb], hb[:], "lr_hst", reads=["lr_h", "lr_h0"], writes=[("oT", 8 + cb)])
        load_x(0)
        load_y(0)
        conv(0)
        if H > 1:
            load_x(1)
        for cb in range(H):
            gelu(cb)
            if cb + 1 < H:
                load_y(cb + 1)
            run_tiles(cb)
            if cb + 1 < H:
                conv(cb + 1)
                if cb + 2 < H:
                    load_x(cb + 2)
            fin(cb)
    S.barrier()


def phase_outproj(K, l):
    S, nc = K.S, K.nc
    AX = mybir.AxisListType
    with ExitStack() as st:
        sb = lambda sh, dt: _alloc(nc, st, "sb", sh, dt)
        wo = sb([128, KC, D], BF16)
        ob = [sb([128, KC, 128], F32) for _ in range(4)]
        sq = [sb([128, KC, 128], F32) for _ in range(2)]
        s8 = [sb([128, 2, 128], F32) for _ in range(2)]
        mT = [sb([128, KC, 128], BF16) for _ in range(2)]
        hin = [sb([128, D], F32) for _ in range(3)]
        ha = [sb([128, D], F32) for _ in range(2)]
        ya = [sb([128, 512], F32) for _ in range(2)]
        yo = [sb([128, D], F32) for _ in range(2)]
        rsd = [sb([128, 4], F32) for _ in range(3)]
        pss = [_alloc(nc, st, "ps", [128, 512], F32) for _ in range(3)]
        pa = [_alloc(nc, st, "ps", [128, 512], F32) for _ in range(2)]
        pl = [_alloc(nc, st, "ps", [128, 512], F32) for _ in range(2)]
        V = K.vec[l]
        wkeys = K.cast_keys(l, "wout", 2048)
        for q in range(4):
            S.dma("sp", wo[:, 4 * q:4 * q + 4, :],
                  K.w_out_b[l].rearrange("(p a) n -> p (a n)", a=KC)[:, 4 * q * D:(4 * q + 4) * D].rearrange("p (k n) -> p k n", k=4),
                  ("op_wo", q), reads=wkeys, writes=[("op_wo", q)])
        nx = [0]

        def load_ob(i):
            r = i % 4
            S.dma("sp", ob[r][:], K.oT[:, :, i * 128:(i + 1) * 128].rearrange("k p t -> p k t"), ("op_ob", r),
                  reads=[("oT", k) for k in range(16)], writes=[("op_ob", r)])

        def load_hin(i):
            r = i % 3
            S.dma("sp", hin[r][:], K.h[i * 128:(i + 1) * 128, :], ("op_hin", r), reads=[("h", i)], writes=[("op_hin", r)])

        def prepA(i):
            s = i % 2
            r = i % 4
            u = i % 3
            S.op("act", lambda h, s=s, r=r: h.activation(out=sq[s][:], in_=ob[r][:], func=AF.Square), reads=[("op_ob", r)], writes=[("op_sq", s)])
            for g in range(2):
                S.op("dve", lambda h, s=s, g=g: h.tensor_reduce(out=s8[s][:, g, :], in_=sq[s][:, 8 * g:8 * g + 8, :].rearrange("p k t -> p t k"),
                                                               axis=AX.X, op=ALU.add),
                     reads=[("op_sq", s)], writes=[("op_s8", s, g)])
                S.op("pe", lambda h, s=s, u=u, g=g: h.matmul(pss[u][:, 2 * g:2 * g + 2], s8[s][:, g, :], K.onesf2[:], start=True, stop=True),
                     reads=[("op_s8", s, g), "ones"], writes=[("op_pss", u)])
            S.op("act", lambda h, u=u: h.activation(out=rsd[u][:], in_=pss[u][:, 0:4], func=AF.Sqrt, scale=1.0 / 1024.0, bias=RMS_EPS),
                 reads=[("op_pss", u)], writes=[("op_rs", u)])
            S.op("dve", lambda h, u=u: h.reciprocal(rsd[u][:], rsd[u][:]), reads=[("op_rs", u)], writes=[("op_rs", u)])

        def prepB(i):
            s = i % 2
            r = i % 4
            r3 = i % 3
            for kc in range(KC):
                eng = "dve" if kc % 2 == 0 else "pool"
                S.op(eng, lambda h, s=s, r=r, kc=kc: h.tensor_scalar(mT[s][:, kc, :], ob[r][:, kc, :], V[:, V_G + kc:V_G + kc + 1], 0.0, ALU.mult, ALU.add),
                     reads=[("op_ob", r), ("vec", l)], writes=[("op_mT", s, kc)])
            S.op("pool", lambda h, s=s, r3=r3: h.tensor_scalar(ha[s][:], hin[r3][:], ALPHA, 0.0, ALU.mult, ALU.add), reads=[("op_hin", r3)], writes=[("op_ha", s)])

        def main(i):
            s = i % 2
            u = i % 3
            for n in range(4):
                x = nx[0] % 2
                nx[0] += 1
                for g, pp in ((0, pa), (1, pl)):
                    for kk in range(8):
                        kc = 8 * g + kk
                        S.op("pe", lambda h, pp=pp, x=x, s=s, kc=kc, kk=kk, n=n: h.matmul(
                            pp[x][:], mT[s][:, kc, :], wo[:, kc, n * 512:(n + 1) * 512], start=(kk == 0), stop=(kk == 7)),
                            reads=[("op_mT", s, kc), ("op_wo", kc // 4)], writes=[("op_p", g, x)])
                S.op("act", lambda h, x=x, u=u: h.activation(out=ya[x][:], in_=pa[x][:], func=AF.Copy, scale=rsd[u][:, 0:1]),
                     reads=[("op_p", 0, x), ("op_rs", u)], writes=[("op_ya", x)])
                S.op("dve", lambda h, x=x, u=u: h.scalar_tensor_tensor(out=ya[x][:], in0=pl[x][:], scalar=rsd[u][:, 2:3], in1=ya[x][:],
                                                                        op0=ALU.mult, op1=ALU.add),
                     reads=[("op_p", 1, x), ("op_rs", u), ("op_ya", x)], writes=[("op_ya", x)])
                S.op("pool", lambda h, x=x, s=s, n=n: h.tensor_tensor(yo[s][:, n * 512:(n + 1) * 512], ya[x][:], ha[s][:, n * 512:(n + 1) * 512], ALU.add),
                     reads=[("op_ya", x), ("op_ha", s)], writes=[("op_yo", s, n)])
            S.dma("sp", K.ypre[i * 128:(i + 1) * 128, :], yo[s][:], ("op_yst", s), reads=[("op_yo", s, n) for n in range(4)], writes=[("ypre", i)])
        load_ob(0)
        load_ob(1)
        load_ob(2)
        load_hin(0)
        load_hin(1)
        prepA(0)
        prepA(1)
        prepB(0)
        for i in range(NB):
            if i + 3 < NB:
                load_ob(i + 3)
            if i + 2 < NB:
                load_hin(i + 2)
                prepA(i + 2)
            if i + 1 < NB:
                prepB(i + 1)
            main(i)
    S.barrier()


def phase_ffn_up(K, l):
    S, nc = K.S, K.nc
    with ExitStack() as st:
        sb = lambda sh, dt: _alloc(nc, st, "sb", sh, dt)
        hTt = [sb([128, KC, 512], BF16) for _ in range(2)]
        wg = [sb([128, KC, 128], BF16) for _ in range(3)]
        wv = [sb([128, KC, 128], BF16) for _ in range(3)]
        ug = [sb([128, 2 + 512], F32) for _ in range(2)]
        uv = [sb([128, 2 + 512], F32) for _ in range(2)]
        cg = [sb([128, 512], F32) for _ in range(2)]
        cv = [sb([128, 512], F32) for _ in range(2)]
        sg = [sb([128, 512], F32) for _ in range(2)]
        at = [sb([128, 512], BF16) for _ in range(3)]
        halo = sb([128, 88, 2], F32)
        pg = [_alloc(nc, st, "ps", [128, 512], F32) for _ in range(2)]
        pv = [_alloc(nc, st, "ps", [128, 512], F32) for _ in range(2)]
        V = K.vec[l]
        S.op("pool", lambda h: h.memset(halo[:], 0.0), writes=[("fu_halo", j) for j in range(NHC)])
        tiles = token_tiles()
        seq = [(ti, j) for ti in range(len(tiles)) for j in range(NHC)]
        wkeys = K.cast_keys(l, "wup", 88 * 128)

        def load_w(n):
            ti, j = seq[n]
            S.dma("sp", wg[n % 3][:], K.w_up_b[l][j * 128:(j + 1) * 128, :].rearrange("p (k j) -> p k j", k=KC),
                  ("fu_wg", n % 3), reads=wkeys, writes=[("fu_wg", n % 3)])
            S.dma("sp", wv[n % 3][:], K.w_up_b[l][(NHC + j) * 128:(NHC + j + 1) * 128, :].rearrange("p (k j) -> p k j", k=KC),
                  ("fu_wv", n % 3), reads=wkeys, writes=[("fu_wv", n % 3)])
        load_w(0)
        load_w(1)
        n = 0
        fcw = lambda c, k: V[:, V_FCW + c * 3 + k:V_FCW + c * 3 + k + 1]
        fcb = lambda c: V[:, V_FCB + c:V_FCB + c + 1]
        for ti, (t0, tw) in enumerate(tiles):
            ts = ti % 2
            S.dma("sp", hTt[ts][:, :, 0:tw], K.hT[:, :, t0:t0 + tw].rearrange("k p t -> p k t"), ("fu_h", ts),
                  reads=[("hT", ti)], writes=[("fu_h", ts)])
            for j in range(NHC):
                if n + 2 < len(seq):
                    load_w(n + 2)
                x = n % 2
                w3 = n % 3
                for (pp, ww, nm) in ((pg, wg, "fu_wg"), (pv, wv, "fu_wv")):
                    for kc in range(KC):
                        S.op("pe", lambda h, pp=pp, ww=ww, x=x, kc=kc, ts=ts, tw=tw, w3=w3: h.matmul(
                            pp[x][:, 0:tw], ww[w3][:, kc, :], hTt[ts][:, kc, 0:tw], start=(kc == 0), stop=(kc == KC - 1)),
                            reads=[(nm, w3), ("fu_h", ts)], writes=[(nm + "_p", x)])
                S.op("act", lambda h, x=x, tw=tw: h.copy(ug[x][:, 2:2 + tw], pg[x][:, 0:tw]), reads=[("fu_wg_p", x)], writes=[("fu_ug", x)])
                S.op("act", lambda h, x=x, tw=tw: h.copy(uv[x][:, 2:2 + tw], pv[x][:, 0:tw]), reads=[("fu_wv_p", x)], writes=[("fu_uv", x)])
                S.op("pool", lambda h, x=x, j=j: h.tensor_copy(ug[x][:, 0:2], halo[:, j, :]), reads=[("fu_halo", j)], writes=[("fu_ugh", x)])
                S.op("pool", lambda h, x=x, j=j: h.tensor_copy(uv[x][:, 0:2], halo[:, NHC + j, :]), reads=[("fu_halo", j)], writes=[("fu_uvh", x)])
                S.op("pool", lambda h, x=x, j=j, tw=tw: h.tensor_copy(halo[:, j, :], ug[x][:, tw:tw + 2]), reads=[("fu_ug", x), ("fu_ugh", x)], writes=[("fu_halo", j)])
                S.op("pool", lambda h, x=x, j=j, tw=tw: h.tensor_copy(halo[:, NHC + j, :], uv[x][:, tw:tw + 2]), reads=[("fu_uv", x), ("fu_uvh", x)], writes=[("fu_halo", j)])
                for (uu, cc, ch, nm) in ((ug, cg, j, "g"), (uv, cv, NHC + j, "v")):
                    S.op("dve", lambda h, uu=uu, cc=cc, ch=ch, x=x, tw=tw: h.tensor_scalar(
                        cc[x][:, 0:tw], uu[x][:, 2:2 + tw], fcw(ch, 2), fcb(ch), ALU.mult, ALU.add),
                        reads=[("fu_u" + nm, x), ("vec", l)], writes=[("fu_c" + nm, x)])
                    for k in (1, 0):
                        S.op("dve", lambda h, uu=uu, cc=cc, ch=ch, x=x, tw=tw, k=k: h.scalar_tensor_tensor(
                            out=cc[x][:, 0:tw], in0=uu[x][:, k:k + tw], scalar=fcw(ch, k), in1=cc[x][:, 0:tw], op0=ALU.mult, op1=ALU.add),
                            reads=[("fu_u" + nm, x), ("fu_u" + nm + "h", x), ("fu_c" + nm, x), ("vec", l)], writes=[("fu_c" + nm, x)])
                S.op("act", lambda h, x=x, tw=tw: h.activation(out=sg[x][:, 0:tw], in_=cg[x][:, 0:tw], func=AF.Silu),
                     reads=[("fu_cg", x)], writes=[("fu_sg", x)])
                S.op("pool", lambda h, x=x, w3=w3, tw=tw: h.tensor_tensor(at[w3][:, 0:tw], sg[x][:, 0:tw], cv[x][:, 0:tw], ALU.mult),
                     reads=[("fu_sg", x), ("fu_cv", x)], writes=[("fu_at", w3)])
                S.dma("sp", K.aT[j, :, t0:t0 + tw], at[w3][:, 0:tw], ("fu_at", w3), reads=[("fu_at", w3)], writes=[("aT", j, ti)])
                n += 1
    S.barrier()


def phase_ffn_down(K, l):
    S, nc = K.S, K.nc
    NW = 512
    with ExitStack() as st:
        sb = lambda sh, dt: _alloc(nc, st, "sb", sh, dt)
        at = [sb([128, NHC, 512], BF16) for _ in range(2)]
        wd = [sb([128, NHC, NW], BF16) for _ in range(2)]
        hin = [sb([128, NW], F32) for _ in range(2)]
        yo = [sb([128, NW], F32) for _ in range(2)]
        pd = [_alloc(nc, st, "ps", [128, NW], F32) for _ in range(4)]
        tiles = token_tiles()
        wkeys = K.cast_keys(l, "wdn", FH)
        nchunk = D // NW
        seq = [(ti, n) for ti in range(len(tiles)) for n in range(nchunk)]

        def load_w(q):
            ti, n = seq[q]
            S.dma("sp", wd[q % 2][:], K.w_down_b[l].rearrange("(p a) n -> p (a n)", a=NHC).rearrange("p (j n) -> p j n", j=NHC)[:, :, n * NW:(n + 1) * NW],
                  ("fd_wd", q % 2), reads=wkeys, writes=[("fd_wd", q % 2)])

        def load_at(ti):
            t0, tw = tiles[ti]
            S.dma("sp", at[ti % 2][:, :, 0:tw], K.aT[:, :, t0:t0 + tw].rearrange("j p t -> p j t"), ("fd_at", ti % 2),
                  reads=[("aT", j, ti) for j in range(NHC)], writes=[("fd_at", ti % 2)])
        load_at(0)
        load_w(0)
        q = 0
        m = 0
        for ti, (t0, tw) in enumerate(tiles):
            ts = ti % 2
            if ti + 1 < len(tiles):
                load_at(ti + 1)
            for n in range(nchunk):
                if q + 1 < len(seq):
                    load_w(q + 1)
                for bj in range(tw // 128):
                    i = t0 // 128 + bj
                    x = m % 4
                    y = m % 2
                    m += 1
                    S.dma("sp", hin[y][:], K.h[i * 128:(i + 1) * 128, n * NW:(n + 1) * NW], ("fd_hin", y), reads=[("h", i)], writes=[("fd_hin", y)])
                    for hc in range(NHC):
                        S.op("pe", lambda h, x=x, ts=ts, hc=hc, bj=bj, q=q: h.matmul(
                            pd[x][:], at[ts][:, hc, bj * 128:(bj + 1) * 128], wd[q % 2][:, hc, :], start=(hc == 0), stop=(hc == NHC - 1)),
                            reads=[("fd_at", ts), ("fd_wd", q % 2)], writes=[("fd_pd", x)])
                    S.op("dve", lambda h, x=x, y=y: h.scalar_tensor_tensor(out=yo[y][:], in0=hin[y][:], scalar=ALPHA, in1=pd[x][:],
                                                                            op0=ALU.mult, op1=ALU.add),
                         reads=[("fd_pd", x), ("fd_hin", y)], writes=[("fd_yo", y)])
                    S.dma("sp", K.ypre[i * 128:(i + 1) * 128, n * NW:(n + 1) * NW], yo[y][:], ("fd_yo", y), reads=[("fd_yo", y)], writes=[("ypre", i)])
                q += 1
    S.barrier()


def build_nc(nlayers=DEPTH, debug=False, stop=None):
    nc = bass.Bass("TRN2", target_bir_lowering=False)
    K = Ctx()
    K.nc = nc
    K.nlayers = nlayers
    dt_in = lambda name, shape: nc.dram_tensor(name, list(shape), F32, kind="ExternalInput").ap()
    kind_dbg = "ExternalOutput" if debug else "Internal"
    scr = lambda name, shape, dt: nc.dram_tensor(name, list(shape), dt, kind=kind_dbg).ap()
    K.x = dt_in("x", [SEQ, D])
    K.meta = dt_in("meta", [NMETA, D])
    K.lng = dt_in("lng", [9, D])
    K.lnb = dt_in("lnb", [9, D])
    NL = nlayers
    K.w_in = dt_in("w_in_t", [NL, NIN * 128, KC * 128])
    K.w_out = dt_in("w_out_t", [NL, 2048, 2048])
    K.w_up = dt_in("w_up_t", [NL, 88 * 128, KC * 128])
    K.w_down = dt_in("w_down_t", [NL, FH, 2048])
    K.lru_wa = dt_in("lru_wa", [NL, H, 128, 128])
    K.lru_wi = dt_in("lru_wi", [NL, H, 128, 128])
    K.vecd = dt_in("vecp", [NL, 128, NV])
    K.out = nc.dram_tensor("out", [SEQ, D], F32, kind="ExternalOutput").ap()
    wsc = lambda name, shape: nc.dram_tensor(name, list(shape), BF16, kind="Internal").ap()
    K.w_in_b = wsc("w_in_b", [NL, NIN * 128, KC * 128])
    K.w_out_b = wsc("w_out_b", [NL, 2048, 2048])
    K.w_up_b = wsc("w_up_b", [NL, 88 * 128, KC * 128])
    K.w_down_b = wsc("w_down_b", [NL, FH, 2048])
    K.h = scr("h_s", [TP, D], F32)
    K.ypre = scr("ypre_s", [TP, D], F32)
    K.hT = scr("hT_s", [KC, 128, TP], BF16)
    K.qkvT = scr("qkvT_s", [24, 128, TP], BF16)
    K.pf = scr("pf_s", [16, 128, TP], F32)
    K.oT = scr("oT_s", [16, 128, TP], F32)
    K.aT = scr("aT_s", [NHC, 128, TP], BF16)
    K.w_out = K.w_out
    with ExitStack() as st:
        S = Sched(nc, st)
        K.S = S
        sb = lambda sh, dt: _alloc(nc, st, "sb", sh, dt)
        K.ones = sb([128, 512], F32)
        K.onesf2 = sb([128, 2], F32)
        identf = sb([128, 128], F32)
        K.ident = sb([128, 128], BF16)
        K.mtrif = sb([128, 128], F32)
        K.m0f = sb([128, 128], F32)
        K.vec = [sb([128, NV], F32) for _ in range(nlayers)]
        S.op("pool", lambda h: h.memset(K.ones[:], 1.0), writes=["ones"])
        S.op("pool", lambda h: h.memset(K.onesf2[:], 1.0), writes=["ones"])
        S.op("pool", lambda h: h.memset(identf[:], 0.0), writes=["identf"])
        S.op("pool", lambda h: h.affine_select(out=identf[:], in_=K.ones[:, 0:128], pattern=[[-1, 128]], compare_op=ALU.is_equal,
                                                fill=0.0, base=0, channel_multiplier=1), reads=["ones"], writes=["identf"])
        S.op("dve", lambda h: h.tensor_copy(K.ident[:], identf[:]), reads=["identf"], writes=["ident"])
        S.op("pool", lambda h: h.affine_select(out=K.mtrif[:], in_=K.ones[:, 0:128], pattern=[[-1, 128]], compare_op=ALU.is_gt,
                                                fill=0.0, base=0, channel_multiplier=1), reads=["ones"], writes=["masks"])
        S.op("pool", lambda h: h.tensor_copy(K.m0f[:], K.mtrif[:]), reads=["masks"], writes=["masks0"])
        S.op("pool", lambda h: h.memset(K.m0f[:, 0:NPAD], 0.0), reads=["masks0"], writes=["masks0"])
        K.zeros = sb([128, 512], F32)
        K.imtrif = sb([128, 128], F32)
        K.im0f = sb([128, 128], F32)
        S.op("pool", lambda h: h.memset(K.zeros[:], 0.0), writes=["zeros"])
        S.op("dve", lambda h: h.tensor_scalar(K.imtrif[:], K.mtrif[:], -1.0, 1.0, ALU.mult, ALU.add), reads=["masks"], writes=["imasks"])
        S.op("dve", lambda h: h.tensor_scalar(K.im0f[:], K.m0f[:], -1.0, 1.0, ALU.mult, ALU.add), reads=["masks0"], writes=["imasks"])
        for l in range(nlayers):
            S.dma("sp", K.vec[l][:], K.vecd[l], ("vecld", l), writes=[("vec", l)])
        S.barrier()
        phase_cast(K, [0])

        def src0(S_, i, tile, key):
            if i == 0:
                S_.op("pool", lambda h: h.memset(tile[:], 0.0), writes=[key])
                S_.dma("sp", tile[NPAD:128, :], K.meta, ("ln_yld", key), writes=[key])
            else:
                S_.dma("sp", tile[:], K.x[(i - 1) * 128:i * 128, :], ("ln_yld", key), writes=[key])

        def srcy(S_, i, tile, key):
            S_.dma("sp", tile[:], K.ypre[i * 128:(i + 1) * 128, :], ("ln_yld", key), reads=[("ypre", i)], writes=[key])
        dsth = lambda i: K.h[i * 128:(i + 1) * 128, :]
        dstout = lambda i: (None if i == 0 else K.out[(i - 1) * 128:i * 128, :])
        phases = []
        phases.append(("ln0", lambda: phase_ln(K, 0, src0, dsth)))
        for l in range(nlayers):
            last = (l == nlayers - 1)
            phases.append(("inproj%d" % l, lambda l=l: phase_inproj(K, l)))
            phases.append(("attn%d" % l, lambda l=l: phase_attn(K, l)))
            if l == 0 and nlayers > 1:
                phases.append(("cast_rest", lambda: phase_cast(K, list(range(1, nlayers)))))
            phases.append(("lru%d" % l, lambda l=l: phase_lru(K, l)))
            phases.append(("outproj%d" % l, lambda l=l: phase_outproj(K, l)))
            phases.append(("ln1_%d" % l, lambda l=l: phase_ln(K, 1 + 2 * l, srcy, dsth)))
            phases.append(("ffnup%d" % l, lambda l=l: phase_ffn_up(K, l)))
            phases.append(("ffndn%d" % l, lambda l=l: phase_ffn_down(K, l)))
            if last and not debug:
                phases.append(("ln2_%d" % l, lambda l=l: phase_ln(K, 2 + 2 * l, srcy, dstout, write_hT=False)))
            else:
                phases.append(("ln2_%d" % l, lambda l=l: phase_ln(K, 2 + 2 * l, srcy, dsth)))
        for name, fn in phases:
            fn()
            if stop is not None and name == stop:
                break
        S.finish()
        K.ninstr = dict(S.ninstr)
        K.nsem = S.nsem
    return nc, K


def host_layout(inputs, nl=DEPTH):
    f = lambda a: np.ascontiguousarray(np.asarray(a, dtype=np.float32))
    w_in = f(inputs["w_in"])
    w_in_t = np.ascontiguousarray(w_in.reshape(DEPTH, KC, 128, NIN, 128).transpose(0, 3, 2, 1, 4)).reshape(DEPTH, NIN * 128, KC * 128)
    w_out = f(inputs["w_out"])
    w_out_t = np.ascontiguousarray(w_out.reshape(DEPTH, KC, 128, D).transpose(0, 2, 1, 3)).reshape(DEPTH, 128, KC * D).reshape(DEPTH, 2048, 2048)
    w_up = f(inputs["ffn_w_up"])
    w_up_t = np.ascontiguousarray(w_up.reshape(DEPTH, KC, 128, 88, 128).transpose(0, 3, 2, 1, 4)).reshape(DEPTH, 88 * 128, KC * 128)
    w_dn = f(inputs["ffn_w_down"])
    w_dn_t = np.ascontiguousarray(w_dn.reshape(DEPTH, NHC, 128, D).transpose(0, 2, 1, 3)).reshape(DEPTH, 128, NHC * D).reshape(DEPTH, FH, 2048)
    vec = np.zeros((DEPTH, 128, NV), np.float32)
    pc = lambda v, n: v.reshape(DEPTH, n, 128).transpose(0, 2, 1)
    vec[:, :, V_CW:V_CW + 32] = f(inputs["lru_conv_w"]).reshape(DEPTH, 4, 8, 128).transpose(0, 3, 2, 1).reshape(DEPTH, 128, 32)
    vec[:, :, V_CB:V_CB + 8] = pc(f(inputs["lru_conv_b"]), 8)
    vec[:, :, V_BA:V_BA + 8] = pc(f(inputs["lru_b_a"]), 8)
    vec[:, :, V_BI:V_BI + 8] = pc(f(inputs["lru_b_i"]), 8)
    vec[:, :, V_LAM:V_LAM + 8] = pc(f(inputs["lru_lambda"]), 8)
    gcat = np.concatenate([f(inputs["g_attn"]), f(inputs["g_lru"])], axis=1)
    vec[:, :, V_G:V_G + 16] = pc(gcat, 16)
    vec[:, :, V_FCW:V_FCW + 264] = f(inputs["ffn_conv_w"]).reshape(DEPTH, 3, 88, 128).transpose(0, 3, 2, 1).reshape(DEPTH, 128, 264)
    vec[:, :, V_FCB:V_FCB + 88] = pc(f(inputs["ffn_conv_b"]), 88)
    lng = np.zeros((9, D), np.float32)
    lnb = np.zeros((9, D), np.float32)
    lng[0] = f(inputs["ln0_g"])
    lnb[0] = f(inputs["ln0_b"])
    for l in range(DEPTH):
        lng[1 + 2 * l] = f(inputs["ln1_g"])[l]
        lnb[1 + 2 * l] = f(inputs["ln1_b"])[l]
        lng[2 + 2 * l] = f(inputs["ln2_g"])[l]
        lnb[2 + 2 * l] = f(inputs["ln2_b"])[l]
    shared = {
        "meta": f(inputs["meta_tokens"]), "lng": lng, "lnb": lnb,
        "w_in_t": w_in_t[:nl], "w_out_t": w_out_t[:nl], "w_up_t": w_up_t[:nl], "w_down_t": w_dn_t[:nl],
        "lru_wa": f(inputs["lru_w_a"])[:nl], "lru_wi": f(inputs["lru_w_i"])[:nl], "vecp": vec[:nl],
    }
    return shared


_cache = {}


def kernel(**inputs):
    x = np.asarray(inputs["x"], dtype=np.float32)
    B = x.shape[0]
    shared = host_layout(inputs)
    if "nc" not in _cache:
        _cache["nc"] = build_nc()[0]
    nc = _cache["nc"]
    in_maps = []
    for c in range(B):
        m = dict(shared)
        m["x"] = np.ascontiguousarray(x[c])
        in_maps.append(m)
    res = run_bass_kernel_spmd(nc, in_maps, core_ids=list(range(B)))
    out = np.stack([np.asarray(res.results[c]["out"], dtype=np.float32) for c in range(B)], axis=0)
    return out
```

```python
import numpy as np
from contextlib import ExitStack
import concourse.bass as bass
import concourse.mybir as mybir
from concourse.bass_utils import run_bass_kernel_spmd

F32 = mybir.dt.float32
BF16 = mybir.dt.bfloat16
AF = mybir.ActivationFunctionType
ALU = mybir.AluOpType

D = 2048
SEQ = 4096
NMETA = 16
DEPTH = 4
TP = 4224
NPAD = 112
NB = 33
KC = 16
H = 8
NIN = 40
FH = 5632
NHC = 44
ALPHA = float((2 * DEPTH) ** 0.25)
SCALE = float(128 ** -0.5)
LN_EPS = 1e-5
RMS_EPS = 1e-6
NV = 432
NWARM = 0
V_CW, V_CB, V_BA, V_BI, V_LAM, V_G, V_FCW, V_FCB = 0, 32, 40, 48, 56, 64, 80, 344


class Tok:
    __slots__ = ("w", "r", "sem", "cnt", "name")

    def __init__(self, name):
        self.w = {}
        self.r = {}
        self.sem = None
        self.cnt = 0
        self.name = name


class Sched:
    ENG = ("pe", "act", "dve", "pool", "sp")

    def __init__(self, nc, stack):
        self.nc = nc
        self.stack = stack
        self.ops = {e: [] for e in self.ENG}
        self.esem = {}
        self.ecnt = {e: 0 for e in self.ENG}
        self.waited = {e: {} for e in self.ENG}
        for e in self.ENG:
            self.esem[e] = stack.enter_context(nc.semaphore("s_" + e))
        self.nsem = 5
        self.toks = {}
        self.dtoks = []
        self.ninstr = {e: 0 for e in self.ENG}

    def tok(self, key):
        if isinstance(key, Tok):
            return key
        t = self.toks.get(key)
        if t is None:
            t = Tok(key)
            self.toks[key] = t
        return t

    def _deps(self, eng, reads, writes):
        deps = {}

        def upd(d):
            for k, sv in d.items():
                o = deps.get(k)
                if o is None or o[1] < sv[1]:
                    deps[k] = sv
        for t in reads:
            upd(t.w)
        for t in writes:
            upd(t.w)
            upd(t.r)
        out = []
        wd = self.waited[eng]
        for k, (s, v) in deps.items():
            if eng == "pe" and k == "pe":
                continue
            if wd.get(k, 0) >= v:
                continue
            wd[k] = v
            out.append((s, v))
        return out

    def _commit(self, key, sem, val, reads, writes):
        for t in reads:
            o = t.r.get(key)
            if o is None or o[1] < val:
                t.r[key] = (sem, val)
        for t in writes:
            t.w = {key: (sem, val)}
            t.r = {}

    def op(self, eng, fn, reads=(), writes=()):
        reads = [self.tok(k) for k in reads]
        writes = [self.tok(k) for k in writes]
        waits = self._deps(eng, reads, writes)
        self.ecnt[eng] += 1
        val = self.ecnt[eng]
        sem = self.esem[eng]

        def emit(h, waits=waits, fn=fn, sem=sem):
            for (s, v) in waits:
                h.wait_ge(s, v)
            fn(h).then_inc(sem, 1)
        self.ops[eng].append(emit)
        self.ninstr[eng] += 1 + len(waits)
        self._commit(eng, sem, val, reads, writes)

    def dma(self, q, out, in_, semtok, reads=(), writes=()):
        reads = [self.tok(k) for k in reads]
        writes = [self.tok(k) for k in writes]
        st = self.tok(semtok)
        if st.sem is None:
            st.sem = self.stack.enter_context(self.nc.semaphore("d%d" % self.nsem))
            self.nsem += 1
            self.dtoks.append(st)
        waits = self._deps(q, reads, writes)
        st.cnt += 16
        val = st.cnt
        sem = st.sem
        key = "D:%s" % (st.name,)

        def emit(h, waits=waits, sem=sem, out=out, in_=in_):
            for (s, v) in waits:
                h.wait_ge(s, v)
            h.dma_start(out=out, in_=in_).then_inc(sem, 16)
        self.ops[q].append(emit)
        self.ninstr[q] += 1 + len(waits)
        self._commit(key, sem, val, reads, writes)

    def barrier(self):
        targets = [(e, self.esem[e], self.ecnt[e]) for e in self.ENG]
        targets += [("D:%s" % (t.name,), t.sem, t.cnt) for t in self.dtoks]
        for eng in self.ENG:
            waits = []
            wd = self.waited[eng]
            for (k, s, v) in targets:
                if v == 0 or wd.get(k, 0) >= v:
                    continue
                wd[k] = v
                waits.append((s, v))
            if waits:
                def emit(h, waits=waits):
                    for (s, v) in waits:
                        h.wait_ge(s, v)
                self.ops[eng].append(emit)
                self.ninstr[eng] += len(waits)

    def finish(self):
        self.barrier()
        nc = self.nc
        ops = self.ops
        with nc.Block() as block:
            @block.tensor
            def _(h):
                for f in ops["pe"]:
                    f(h)

            @block.scalar
            def _(h):
                for f in ops["act"]:
                    f(h)

            @block.vector
            def _(h):
                for f in ops["dve"]:
                    f(h)

            @block.gpsimd
            def _(h):
                for f in ops["pool"]:
                    f(h)

            @block.sync
            def _(h):
                for f in ops["sp"]:
                    f(h)


class Ctx:
    pass


_uid = [0]


def _alloc(nc, st, kind, shape, dt):
    _uid[0] += 1
    name = "%s%d" % (kind, _uid[0])
    if kind == "ps":
        return st.enter_context(nc.psum_tensor(name, list(shape), dt))
    return st.enter_context(nc.sbuf_tensor(name, list(shape), dt))


def token_tiles():
    tiles = [(0, 128)]
    for i in range(8):
        tiles.append((128 + 512 * i, 512))
    return tiles


def phase_cast(K, layers):
    S = K.S
    K.cast_keys = lambda l, kind, rows: [("wb", l, kind, r0) for r0 in range(0, rows, 512)]
    for l in layers:
        for kind, src, dst, rows in (("win", K.w_in, K.w_in_b, NIN * 128), ("wout", K.w_out, K.w_out_b, 2048),
                                     ("wup", K.w_up, K.w_up_b, 88 * 128), ("wdn", K.w_down, K.w_down_b, FH)):
            s2 = src[l]
            d2 = dst[l]
            step = 512
            for r0 in range(0, rows, step):
                r1 = min(rows, r0 + step)
                S.dma("pool", d2[r0:r1, :], s2[r0:r1, :], ("cast", l, kind), reads=[], writes=[("wb", l, kind, r0)])


def phase_ln(K, li, src_fn, dst_fn, write_hT=True):
    S, nc = K.S, K.nc
    with ExitStack() as st:
        sb = lambda sh, dt: _alloc(nc, st, "sb", sh, dt)
        gbc = sb([128, D], F32)
        bbc = sb([128, D], F32)
        yb = [sb([128, D], F32) for _ in range(3)]
        xn = [sb([128, D], F32) for _ in range(2)]
        hb = [sb([128, D], F32) for _ in range(2)]
        hbf = [sb([128, D], BF16) for _ in range(2)]
        hTt = [sb([128, KC, 512], BF16) for _ in range(2)]
        stats = [sb([128, 4, 6], F32) for _ in range(2)]
        mv = [sb([128, 2], F32) for _ in range(2)]
        rstd = [sb([128, 1], F32) for _ in range(2)]
        pT = [_alloc(nc, st, "ps", [128, 512], BF16) for _ in range(4)]
        S.dma("sp", gbc[:], K.lng[li].partition_broadcast(128), "ln_g", writes=["ln_gbc"])
        S.dma("sp", bbc[:], K.lnb[li].partition_broadcast(128), "ln_b", writes=["ln_bbc"])
        blocks = []
        for ti, (t0, tw) in enumerate(token_tiles()):
            nblk = tw // 128
            for bj in range(nblk):
                blocks.append((t0 // 128 + bj, ti, bj, nblk, t0, tw))
        npt = [0]

        def stage0(blk):
            i = blk[0]
            src_fn(S, i, yb[i % 3], ("ln_y", i % 3))

        def stage1(blk):
            i, ti, bj, nblk, t0, tw = blk
            s = i % 2
            r = i % 3
            for q in range(4):
                S.op("dve", lambda h, s=s, r=r, q=q: h.bn_stats(stats[s][:, q, :], yb[r][:, q * 512:(q + 1) * 512]),
                     reads=[("ln_y", r)], writes=[("ln_st", s, q)])
            S.op("dve", lambda h, s=s: h.bn_aggr(mv[s][:], stats[s][:]),
                 reads=[("ln_st", s, q) for q in range(4)], writes=[("ln_mv", s)])
            S.op("act", lambda h, s=s: h.activation(out=rstd[s][:], in_=mv[s][:, 1:2], func=AF.Sqrt, bias=LN_EPS),
                 reads=[("ln_mv", s)], writes=[("ln_rs", s)])
            S.op("dve", lambda h, s=s: h.reciprocal(rstd[s][:], rstd[s][:]), reads=[("ln_rs", s)], writes=[("ln_rs", s)])
            S.op("dve", lambda h, s=s, r=r: h.tensor_scalar(xn[s][:], yb[r][:], mv[s][:, 0:1], rstd[s][:], ALU.subtract, ALU.mult),
                 reads=[("ln_y", r), ("ln_mv", s), ("ln_rs", s)], writes=[("ln_xn", s), ("ln_xg", s, 0), ("ln_xg", s, 1)])
            GS = 1280
            S.op("dve", lambda h, s=s: h.tensor_tensor(xn[s][:, 0:GS], xn[s][:, 0:GS], gbc[:, 0:GS], ALU.mult),
                 reads=[("ln_xn", s), "ln_gbc"], writes=[("ln_xg", s, 0)])
            S.op("pool", lambda h, s=s: h.tensor_tensor(xn[s][:, GS:D], xn[s][:, GS:D], gbc[:, GS:D], ALU.mult),
                 reads=[("ln_xn", s), "ln_gbc"], writes=[("ln_xg", s, 1)])
            S.op("pool", lambda h, s=s: h.tensor_tensor(hb[s][:], xn[s][:], bbc[:], ALU.add),
                 reads=[("ln_xn", s), ("ln_xg", s, 0), ("ln_xg", s, 1), "ln_bbc"], writes=[("ln_h", s)])

        def stage2(blk):
            i, ti, bj, nblk, t0, tw = blk
            s = i % 2
            ts = ti % 2
            dst = dst_fn(i)
            if dst is not None:
                S.dma("sp", dst, hb[s][:], ("ln_hst", s), reads=[("ln_h", s)], writes=[("h", i)])
            if not write_hT:
                return
            S.op("act", lambda h, s=s: h.copy(hbf[s][:], hb[s][:]), reads=[("ln_h", s)], writes=[("ln_hbf", s)])
            for g in range(4):
                p = npt[0] % 4
                npt[0] += 1
                for j in range(4):
                    kc = 4 * g + j
                    S.op("pe", lambda h, s=s, p=p, j=j, kc=kc: h.transpose(
                        pT[p][:, j * 128:(j + 1) * 128], hbf[s][:, kc * 128:(kc + 1) * 128], K.ident[:]),
                        reads=[("ln_hbf", s), "ident"], writes=[("ln_pT", p)])
                outap = hTt[ts][:, 4 * g:4 * g + 4, bj * 128:(bj + 1) * 128]
                inap = pT[p][:, 0:512].rearrange("p (a b) -> p a b", a=4)
                if g != 3:
                    S.op("act", lambda h, outap=outap, inap=inap: h.copy(outap, inap),
                         reads=[("ln_pT", p)], writes=[("ln_hT", ts, bj, g)])
                else:
                    S.op("dve", lambda h, outap=outap, inap=inap: h.tensor_copy(outap, inap),
                         reads=[("ln_pT", p)], writes=[("ln_hT", ts, bj, g)])
            if i == 0:
                S.op("pool", lambda h, ts=ts: h.memset(hTt[ts][:, :, 0:NPAD], 0.0),
                     reads=[], writes=[("ln_hT", ts, 0, g) for g in range(4)])
            if bj == nblk - 1:
                S.dma("sp", K.hT[:, :, t0:t0 + tw].rearrange("k p t -> p k t"), hTt[ts][:, :, 0:tw], ("ln_hTst", ts),
                      reads=[("ln_hT", ts, b2, g) for b2 in range(nblk) for g in range(4)], writes=[("hT", ti)])
        stage0(blocks[0])
        stage0(blocks[1])
        stage1(blocks[0])
        for n, blk in enumerate(blocks):
            if n + 2 < len(blocks):
                stage0(blocks[n + 2])
            if n + 1 < len(blocks):
                stage1(blocks[n + 1])
            stage2(blk)
    S.barrier()


def phase_inproj(K, l):
    S, nc = K.S, K.nc
    with ExitStack() as st:
        sb = lambda sh, dt: _alloc(nc, st, "sb", sh, dt)
        hTt = [sb([128, KC, 512], BF16) for _ in range(2)]
        w = [sb([128, KC, 128], BF16) for _ in range(4)]
        ob = [sb([128, 512], BF16) for _ in range(4)]
        of = [sb([128, 512], F32) for _ in range(4)]
        pz = [_alloc(nc, st, "ps", [128, 512], F32) for _ in range(4)]
        tiles = token_tiles()
        seq = [(ti, c) for ti in range(len(tiles)) for c in range(NIN)]
        wkeys = K.cast_keys(l, "win", NIN * 128)

        def load_w(n):
            ti, c = seq[n]
            S.dma("sp", w[n % 4][:], K.w_in_b[l][c * 128:(c + 1) * 128, :].rearrange("p (k j) -> p k j", k=KC),
                  ("ip_w", n % 4), reads=wkeys, writes=[("ip_w", n % 4)])
        for n in range(3):
            load_w(n)
        n = 0
        for ti, (t0, tw) in enumerate(tiles):
            ts = ti % 2
            S.dma("sp", hTt[ts][:, :, 0:tw], K.hT[:, :, t0:t0 + tw].rearrange("k p t -> p k t"), ("ip_h", ts),
                  reads=[("hT", ti)], writes=[("ip_h", ts)])
            for c in range(NIN):
                if n + 3 < len(seq):
                    load_w(n + 3)
                p = n % 4
                for kc in range(KC):
                    S.op("pe", lambda h, p=p, kc=kc, ts=ts, tw=tw, ws=n % 4: h.matmul(
                        pz[p][:, 0:tw], w[ws][:, kc, :], hTt[ts][:, kc, 0:tw], start=(kc == 0), stop=(kc == KC - 1)),
                        reads=[("ip_w", n % 4), ("ip_h", ts)], writes=[("ip_pz", p)])
                if c < 24:
                    dst, dram, key = ob[p], K.qkvT[c, :, t0:t0 + tw], ("qkvT", c, ti)
                else:
                    dst, dram, key = of[p], K.pf[c - 24, :, t0:t0 + tw], ("pf", c - 24, ti)
                if n % 2 == 0:
                    S.op("act", lambda h, dst=dst, p=p, tw=tw: h.copy(dst[:, 0:tw], pz[p][:, 0:tw]),
                         reads=[("ip_pz", p)], writes=[("ip_o", c < 24, p)])
                else:
                    S.op("dve", lambda h, dst=dst, p=p, tw=tw: h.tensor_copy(dst[:, 0:tw], pz[p][:, 0:tw]),
                         reads=[("ip_pz", p)], writes=[("ip_o", c < 24, p)])
                S.dma("sp", dram, dst[:, 0:tw], ("ip_o", c < 24, p), reads=[("ip_o", c < 24, p)], writes=[key])
                n += 1
    S.barrier()


def rev(v):
    pairs = [list(p) for p in v.ap]
    last = pairs[-1]
    off = v.offset + (last[1] - 1) * last[0]
    pairs[-1] = [-last[0], last[1]]
    return bass.AP(v.tensor, off, pairs)


def phase_attn(K, l):
    S, nc = K.S, K.nc
    with ExitStack() as st:
        sb = lambda sh, dt: _alloc(nc, st, "sb", sh, dt)
        qkv = [[sb([128, TP], BF16) for _ in range(3)] for _ in range(2)]
        vtm = sb([128, NB, 128], BF16)
        Qrow = [sb([128, TP + 1], F32) for _ in range(2)]
        dtb = [sb([128, 512], F32) for _ in range(4)]
        Ab = [sb([128, 512], BF16) for _ in range(6)]
        ATb = [sb([128, 4, 128], BF16) for _ in range(2)]
        osb = [sb([128, TP], F32) for _ in range(2)]
        pz = [_alloc(nc, st, "ps", [128, 512], F32) for _ in range(3)]
        pAT = [_alloc(nc, st, "ps", [128, 512], BF16) for _ in range(2)]
        po = [_alloc(nc, st, "ps", [128, 128], F32) for _ in range(2)]
        pjunk = _alloc(nc, st, "ps", [128, 512], F32)
        ntiles = len(token_tiles())
        nz = 0
        na = 0
        nblk = 0

        def load_head(hd):
            s = hd % 2
            for j in range(3):
                S.dma("sp", qkv[s][j][:], K.qkvT[8 * j + hd], ("at_qkv", s, j),
                      reads=[("qkvT", 8 * j + hd, ti) for ti in range(ntiles)], writes=[("at_qkv", s, j)])
        load_head(0)
        for hd in range(H):
            hs = hd % 2
            qT, kT, vT = qkv[hs]
            if hd + 1 < H:
                load_head(hd + 1)
            for g in range(9):
                nb_ = min(4, NB - 4 * g)
                p = na % 2
                na += 1
                for j in range(nb_):
                    blk = 4 * g + j
                    S.op("pe", lambda h, p=p, j=j, blk=blk, vT=vT: h.transpose(
                        pAT[p][:, j * 128:(j + 1) * 128], vT[:, blk * 128:(blk + 1) * 128], K.ident[:]),
                        reads=[("at_qkv", hs, 2), "ident"], writes=[("at_pAT", p)])
                S.op("act", lambda h, p=p, g=g, nb_=nb_: h.copy(
                    vtm[:, 4 * g:4 * g + nb_, :], pAT[p][:, 0:nb_ * 128].rearrange("p (a b) -> p a b", a=nb_)),
                    reads=[("at_pAT", p)], writes=[("at_vtm", g)])
            chunks = []
            for b in range(NB):
                kend = 128 * (b + 1)
                nch = (kend + 511) // 512
                for c in reversed(range(nch)):
                    chunks.append((b, c, nch, kend, nblk % 2))
                nblk += 1

            def stageA(n):
                b, c, nch, kend, qs = chunks[n]
                Q = Qrow[qs]
                c0 = 512 * c
                cw = min(512, kend - c0)
                xp = n % 3
                x = n % 4
                y = n % 6
                mf = K.m0f if b == 0 else K.mtrif
                imf = K.im0f if b == 0 else K.imtrif
                if c == nch - 1:
                    S.op("pool", lambda h, Q=Q, kend=kend: h.memset(Q[:, kend:kend + 1], 1.0), reads=[], writes=[("at_Q", qs, nch)])
                S.op("pe", lambda h, xp=xp, cw=cw, c0=c0, b=b, qT=qT, kT=kT: h.matmul(
                    pz[xp][:, 0:cw], qT[:, b * 128:(b + 1) * 128], kT[:, c0:c0 + cw], start=True, stop=True),
                    reads=[("at_qkv", hs, 0), ("at_qkv", hs, 1)], writes=[("at_pz", xp)])
                for _w in range(NWARM):
                    S.op("pe", lambda h: h.matmul(pjunk[:], K.ident[:], qT[:, 0:512], start=True, stop=True),
                         reads=["ident"], writes=["at_junk"])
                S.op("act", lambda h, x=x, xp=xp, cw=cw: h.activation(out=dtb[x][:, 0:cw], in_=pz[xp][:, 0:cw], func=AF.Sigmoid, scale=-SCALE),
                     reads=[("at_pz", xp)], writes=[("at_d", x)])
                if c == nch - 1:
                    S.op("pool", lambda h, x=x, cw=cw, mf=mf: h.tensor_tensor(
                        dtb[x][:, cw - 128:cw], dtb[x][:, cw - 128:cw], mf[:], ALU.mult),
                        reads=[("at_d", x), "masks"], writes=[("at_d", x)])
                    S.op("pool", lambda h, x=x, cw=cw, imf=imf: h.tensor_tensor(
                        dtb[x][:, cw - 128:cw], dtb[x][:, cw - 128:cw], imf[:], ALU.add),
                        reads=[("at_d", x), "masks"], writes=[("at_d", x)])
                if c == 0 and b > 0:
                    S.op("pool", lambda h, x=x: h.memset(dtb[x][:, 0:NPAD], 1.0), reads=[], writes=[("at_d", x)])
                S.op("dve", lambda h, x=x, cw=cw, c0=c0, Q=Q: h.tensor_tensor_scan(
                    out=rev(Q[:, c0:c0 + cw]), data0=rev(dtb[x][:, 0:cw]), data1=rev(K.zeros[:, 0:cw]),
                    initial=Q[:, c0 + cw:c0 + cw + 1], op0=ALU.mult, op1=ALU.add),
                    reads=[("at_d", x), ("at_Q", qs, c + 1), "zeros"], writes=[("at_Q", qs, c)])
                S.op("pool" if n % 3 != 0 else "dve", lambda h, y=y, cw=cw, c0=c0, Q=Q: h.tensor_tensor(
                    Ab[y][:, 0:cw], Q[:, c0 + 1:c0 + cw + 1], Q[:, c0:c0 + cw], ALU.subtract),
                    reads=[("at_Q", qs, c), ("at_Q", qs, c + 1)], writes=[("at_A", y)])

            def stageB(n):
                b, c, nch, kend, qs = chunks[n]
                c0 = 512 * c
                cw = min(512, kend - c0)
                y = n % 6
                w = n % 2
                z = b % 2
                nkb = cw // 128
                for j in range(nkb):
                    S.op("pe", lambda h, y=y, w=w, j=j: h.transpose(
                        pAT[w][:, j * 128:(j + 1) * 128], Ab[y][:, j * 128:(j + 1) * 128], K.ident[:]),
                        reads=[("at_A", y), "ident"], writes=[("at_pAT", w)])
                outap = ATb[w][:, 0:nkb, :]
                inap = pAT[w][:, 0:nkb * 128].rearrange("p (a b) -> p a b", a=nkb)
                S.op("act", lambda h, outap=outap, inap=inap: h.copy(outap, inap),
                     reads=[("at_pAT", w)], writes=[("at_AT", w)])
                for j in range(nkb):
                    kb = c0 // 128 + j
                    first = (c == nch - 1 and j == 0)
                    lastm = (c == 0 and j == nkb - 1)
                    S.op("pe", lambda h, w=w, j=j, kb=kb, z=z, first=first, lastm=lastm: h.matmul(
                        po[z][:], vtm[:, kb, :], ATb[w][:, j, :], start=first, stop=lastm),
                        reads=[("at_AT", w), ("at_vtm", kb // 4)], writes=[("at_po", z)])
                if c == 0:
                    S.op("act", lambda h, z=z, b=b, hs=hs: h.copy(osb[hs][:, b * 128:(b + 1) * 128], po[z][:]),
                         reads=[("at_po", z)], writes=[("at_o", hs, b)])
            LOOK = 4
            for n in range(min(LOOK, len(chunks))):
                stageA(n)
            for n in range(len(chunks)):
                if n + LOOK < len(chunks):
                    stageA(n + LOOK)
                stageB(n)
            S.dma("sp", K.oT[hd], osb[hs][:], ("at_ost", hs), reads=[("at_o", hs, b) for b in range(NB)], writes=[("oT", hd)])
    S.barrier()


def phase_lru(K, l):
    S, nc = K.S, K.nc
    N0 = NPAD
    NR = TP - NPAD
    with ExitStack() as st:
        sb = lambda sh, dt: _alloc(nc, st, "sb", sh, dt)
        xr = sb([128, 3 + TP], F32)
        yg = sb([128, TP], F32)
        xc = [sb([128, TP], F32) for _ in range(2)]
        xcb = [sb([128, TP], BF16) for _ in range(2)]
        ab = sb([128, TP], F32)
        ub = sb([128, TP], F32)
        hb = sb([128, TP], F32)
        t1 = sb([128, TP], F32)
        waf = sb([128, H, 128], F32)
        wif = sb([128, H, 128], F32)
        wab = sb([128, H, 128], BF16)
        wib = sb([128, H, 128], BF16)
        c1 = sb([128, H], F32)
        rs = [sb([128, 512], F32) for _ in range(2)]
        isb = [sb([128, 512], F32) for _ in range(2)]
        a2 = [sb([128, 512], F32) for _ in range(2)]
        msb = [sb([128, 512], F32) for _ in range(2)]
        pga = [_alloc(nc, st, "ps", [128, 512], F32) for _ in range(2)]
        pgi = [_alloc(nc, st, "ps", [128, 512], F32) for _ in range(2)]
        V = K.vec[l]
        ntiles = len(token_tiles())
        S.dma("sp", waf[:], K.lru_wa[l].rearrange("h i j -> i h j"), "lr_wa", writes=["lr_waf"])
        S.dma("sp", wif[:], K.lru_wi[l].rearrange("h i j -> i h j"), "lr_wi", writes=["lr_wif"])
        S.op("dve", lambda h: h.tensor_copy(wab[:], waf[:]), reads=["lr_waf"], writes=["lr_wab"])
        S.op("dve", lambda h: h.tensor_copy(wib[:], wif[:]), reads=["lr_wif"], writes=["lr_wib"])
        S.op("pool", lambda h: h.memset(xr[:, 0:3], 0.0), writes=["lr_xr0"])
        S.op("pool", lambda h: h.memset(hb[:, 0:N0], 0.0), reads=[], writes=["lr_h0"])
        S.op("act", lambda h: h.activation(out=c1[:], in_=V[:, V_LAM:V_LAM + 8], func=AF.Exp, scale=-1.0), reads=[("vec", l)], writes=["lr_c1"])
        S.op("act", lambda h: h.activation(out=c1[:], in_=c1[:], func=AF.Ln, bias=1.0), reads=["lr_c1"], writes=["lr_c1"])
        S.op("dve", lambda h: h.tensor_scalar(c1[:], c1[:], -8.0, None, ALU.mult), reads=["lr_c1"], writes=["lr_c1"])
        tiles = [(N0 + 512 * i, min(512, TP - (N0 + 512 * i))) for i in range((NR + 511) // 512)]
        nt = [0]

        def load_x(cb):
            S.dma("sp", xr[:, 3:3 + TP], K.pf[cb], "lr_xr", reads=[("pf", cb, ti) for ti in range(ntiles)], writes=["lr_xr"])

        def load_y(cb):
            S.dma("sp", yg[:], K.pf[8 + cb], "lr_yg", reads=[("pf", 8 + cb, ti) for ti in range(ntiles)], writes=["lr_yg"])

        def conv(cb):
            s = cb % 2
            XC = xc[s]
            cwk = [V[:, V_CW + cb * 4 + k:V_CW + cb * 4 + k + 1] for k in range(4)]
            S.op("dve", lambda h, XC=XC, cb=cb, cwk=cwk: h.tensor_scalar(XC[:, N0:TP], xr[:, 3 + N0:3 + TP], cwk[3], V[:, V_CB + cb:V_CB + cb + 1], ALU.mult, ALU.add),
                 reads=["lr_xr", "lr_xr0", ("vec", l)], writes=[("lr_xc", s)])
            for k in (2, 1, 0):
                sh = 3 - k
                S.op("dve", lambda h, XC=XC, k=k, sh=sh, cwk=cwk: h.scalar_tensor_tensor(
                    out=XC[:, N0:TP], in0=xr[:, 3 + N0 - sh:3 + TP - sh], scalar=cwk[k], in1=XC[:, N0:TP], op0=ALU.mult, op1=ALU.add),
                    reads=["lr_xr", "lr_xr0", ("lr_xc", s), ("vec", l)], writes=[("lr_xc", s)])
            S.op("act", lambda h, s=s: h.copy(xcb[s][:, N0:TP], xc[s][:, N0:TP]), reads=[("lr_xc", s)], writes=[("lr_xcb", s)])

        def gelu(cb):
            S.op("dve", lambda h: h.tensor_tensor(t1[:, N0:TP], yg[:, N0:TP], yg[:, N0:TP], ALU.mult), reads=["lr_yg"], writes=["lr_t1"])
            S.op("dve", lambda h: h.tensor_scalar(t1[:, N0:TP], t1[:, N0:TP], 0.044715, 1.0, ALU.mult, ALU.add), reads=["lr_t1"], writes=["lr_t1"])
            S.op("dve", lambda h: h.tensor_tensor(t1[:, N0:TP], t1[:, N0:TP], yg[:, N0:TP], ALU.mult), reads=["lr_t1", "lr_yg"], writes=["lr_t1"])
            S.op("act", lambda h: h.activation(out=t1[:, N0:TP], in_=t1[:, N0:TP], func=AF.Sigmoid, scale=1.5957691216057308),
                 reads=["lr_t1"], writes=["lr_t1"])
            S.op("dve", lambda h: h.tensor_tensor(t1[:, N0:TP], t1[:, N0:TP], yg[:, N0:TP], ALU.mult), reads=["lr_t1", "lr_yg"], writes=["lr_t1"])

        def run_tiles(cb):
            s = cb % 2
            for (t0, tw) in tiles:
                x = nt[0] % 2
                nt[0] += 1
                S.op("pe", lambda h, x=x, t0=t0, tw=tw, cb=cb, s=s: h.matmul(pga[x][:, 0:tw], wab[:, cb, :], xcb[s][:, t0:t0 + tw], start=True, stop=True),
                     reads=["lr_wab", ("lr_xcb", s)], writes=[("lr_pga", x)])
                S.op("pe", lambda h, x=x, t0=t0, tw=tw, cb=cb, s=s: h.matmul(pgi[x][:, 0:tw], wib[:, cb, :], xcb[s][:, t0:t0 + tw], start=True, stop=True),
                     reads=["lr_wib", ("lr_xcb", s)], writes=[("lr_pgi", x)])
                S.op("act", lambda h, x=x, tw=tw, cb=cb: h.activation(out=rs[x][:, 0:tw], in_=pga[x][:, 0:tw], func=AF.Sigmoid,
                                                                       bias=V[:, V_BA + cb:V_BA + cb + 1]),
                     reads=[("lr_pga", x), ("vec", l)], writes=[("lr_r", x)])
                S.op("act", lambda h, x=x, tw=tw, cb=cb: h.activation(out=isb[x][:, 0:tw], in_=pgi[x][:, 0:tw], func=AF.Sigmoid,
                                                                       bias=V[:, V_BI + cb:V_BI + cb + 1]),
                     reads=[("lr_pgi", x), ("vec", l)], writes=[("lr_i", x)])
                S.op("act", lambda h, x=x, t0=t0, tw=tw, cb=cb: h.activation(out=ab[:, t0:t0 + tw], in_=rs[x][:, 0:tw], func=AF.Exp,
                                                                              scale=c1[:, cb:cb + 1]),
                     reads=[("lr_r", x), "lr_c1"], writes=[("lr_a", t0)])
                S.op("pool", lambda h, x=x, t0=t0, tw=tw: h.tensor_tensor(a2[x][:, 0:tw], ab[:, t0:t0 + tw], ab[:, t0:t0 + tw], ALU.mult),
                     reads=[("lr_a", t0)], writes=[("lr_a2", x)])
                S.op("act", lambda h, x=x, tw=tw: h.activation(out=msb[x][:, 0:tw], in_=a2[x][:, 0:tw], func=AF.Sqrt, scale=-1.0, bias=1.0),
                     reads=[("lr_a2", x)], writes=[("lr_m", x)])
                S.op("pool", lambda h, x=x, t0=t0, tw=tw, s=s: h.tensor_tensor(isb[x][:, 0:tw], isb[x][:, 0:tw], xc[s][:, t0:t0 + tw], ALU.mult),
                     reads=[("lr_i", x), ("lr_xc", s)], writes=[("lr_i", x)])
                S.op("pool", lambda h, x=x, t0=t0, tw=tw: h.tensor_tensor(ub[:, t0:t0 + tw], msb[x][:, 0:tw], isb[x][:, 0:tw], ALU.mult),
                     reads=[("lr_i", x), ("lr_m", x)], writes=[("lr_u", t0)])

        def fin(cb):
            S.op("dve", lambda h: h.tensor_tensor_scan(out=hb[:, N0:TP], data0=ab[:, N0:TP], data1=ub[:, N0:TP], initial=0.0,
                                                        op0=ALU.mult, op1=ALU.add),
                 reads=[("lr_a", t0) for (t0, _) in tiles] + [("lr_u", t0) for (t0, _) in tiles], writes=["lr_h"])
            S.op("dve", lambda h: h.tensor_tensor(hb[:, N0:TP], hb[:, N0:TP], t1[:, N0:TP], ALU.mult), reads=["lr_t1", "lr_h"], writes=["lr_h"])
            S.dma("sp", K.oT[8 + cb], hb[:], "lr_hst", reads=["lr_h", "lr_h0"], writes=[("oT", 8 + cb)])
        load_x(0)
        load_y(0)
        conv(0)
        if H > 1:
            load_x(1)
        for cb in range(H):
            gelu(cb)
            if cb + 1 < H:
                load_y(cb + 1)
            run_tiles(cb)
            if cb + 1 < H:
                conv(cb + 1)
                if cb + 2 < H:
                    load_x(cb + 2)
            fin(cb)
    S.barrier()


def phase_outproj(K, l):
    S, nc = K.S, K.nc
    AX = mybir.AxisListType
    with ExitStack() as st:
        sb = lambda sh, dt: _alloc(nc, st, "sb", sh, dt)
        wo = sb([128, KC, D], BF16)
        ob = [sb([128, KC, 128], F32) for _ in range(4)]
        sq = [sb([128, KC, 128], F32) for _ in range(2)]
        s8 = [sb([128, 2, 128], F32) for _ in range(2)]
        mT = [sb([128, KC, 128], BF16) for _ in range(2)]
        hin = [sb([128, D], F32) for _ in range(3)]
        ha = [sb([128, D], F32) for _ in range(2)]
        ya = [sb([128, 512], F32) for _ in range(2)]
        yo = [sb([128, D], F32) for _ in range(2)]
        rsd = [sb([128, 4], F32) for _ in range(3)]
        pss = [_alloc(nc, st, "ps", [128, 512], F32) for _ in range(3)]
        pa = [_alloc(nc, st, "ps", [128, 512], F32) for _ in range(2)]
        pl = [_alloc(nc, st, "ps", [128, 512], F32) for _ in range(2)]
        V = K.vec[l]
        wkeys = K.cast_keys(l, "wout", 2048)
        for q in range(4):
            S.dma("sp", wo[:, 4 * q:4 * q + 4, :],
                  K.w_out_b[l].rearrange("(p a) n -> p (a n)", a=KC)[:, 4 * q * D:(4 * q + 4) * D].rearrange("p (k n) -> p k n", k=4),
                  ("op_wo", q), reads=wkeys, writes=[("op_wo", q)])
        nx = [0]

        def load_ob(i):
            r = i % 4
            S.dma("sp", ob[r][:], K.oT[:, :, i * 128:(i + 1) * 128].rearrange("k p t -> p k t"), ("op_ob", r),
                  reads=[("oT", k) for k in range(16)], writes=[("op_ob", r)])

        def load_hin(i):
            r = i % 3
            S.dma("sp", hin[r][:], K.h[i * 128:(i + 1) * 128, :], ("op_hin", r), reads=[("h", i)], writes=[("op_hin", r)])

        def prepA(i):
            s = i % 2
            r = i % 4
            u = i % 3
            S.op("act", lambda h, s=s, r=r: h.activation(out=sq[s][:], in_=ob[r][:], func=AF.Square), reads=[("op_ob", r)], writes=[("op_sq", s)])
            for g in range(2):
                S.op("dve", lambda h, s=s, g=g: h.tensor_reduce(out=s8[s][:, g, :], in_=sq[s][:, 8 * g:8 * g + 8, :].rearrange("p k t -> p t k"),
                                                               axis=AX.X, op=ALU.add),
                     reads=[("op_sq", s)], writes=[("op_s8", s, g)])
                S.op("pe", lambda h, s=s, u=u, g=g: h.matmul(pss[u][:, 2 * g:2 * g + 2], s8[s][:, g, :], K.onesf2[:], start=True, stop=True),
                     reads=[("op_s8", s, g), "ones"], writes=[("op_pss", u)])
            S.op("act", lambda h, u=u: h.activation(out=rsd[u][:], in_=pss[u][:, 0:4], func=AF.Sqrt, scale=1.0 / 1024.0, bias=RMS_EPS),
                 reads=[("op_pss", u)], writes=[("op_rs", u)])
            S.op("dve", lambda h, u=u: h.reciprocal(rsd[u][:], rsd[u][:]), reads=[("op_rs", u)], writes=[("op_rs", u)])

        def prepB(i):
            s = i % 2
            r = i % 4
            r3 = i % 3
            for kc in range(KC):
                eng = "dve" if kc % 2 == 0 else "pool"
                S.op(eng, lambda h, s=s, r=r, kc=kc: h.tensor_scalar(mT[s][:, kc, :], ob[r][:, kc, :], V[:, V_G + kc:V_G + kc + 1], 0.0, ALU.mult, ALU.add),
                     reads=[("op_ob", r), ("vec", l)], writes=[("op_mT", s, kc)])
            S.op("pool", lambda h, s=s, r3=r3: h.tensor_scalar(ha[s][:], hin[r3][:], ALPHA, 0.0, ALU.mult, ALU.add), reads=[("op_hin", r3)], writes=[("op_ha", s)])

        def main(i):
            s = i % 2
            u = i % 3
            for n in range(4):
                x = nx[0] % 2
                nx[0] += 1
                for g, pp in ((0, pa), (1, pl)):
                    for kk in range(8):
                        kc = 8 * g + kk
                        S.op("pe", lambda h, pp=pp, x=x, s=s, kc=kc, kk=kk, n=n: h.matmul(
                            pp[x][:], mT[s][:, kc, :], wo[:, kc, n * 512:(n + 1) * 512], start=(kk == 0), stop=(kk == 7)),
                            reads=[("op_mT", s, kc), ("op_wo", kc // 4)], writes=[("op_p", g, x)])
                S.op("act", lambda h, x=x, u=u: h.activation(out=ya[x][:], in_=pa[x][:], func=AF.Copy, scale=rsd[u][:, 0:1]),
                     reads=[("op_p", 0, x), ("op_rs", u)], writes=[("op_ya", x)])
                S.op("dve", lambda h, x=x, u=u: h.scalar_tensor_tensor(out=ya[x][:], in0=pl[x][:], scalar=rsd[u][:, 2:3], in1=ya[x][:],
                                                                        op0=ALU.mult, op1=ALU.add),
                     reads=[("op_p", 1, x), ("op_rs", u), ("op_ya", x)], writes=[("op_ya", x)])
                S.op("pool", lambda h, x=x, s=s, n=n: h.tensor_tensor(yo[s][:, n * 512:(n + 1) * 512], ya[x][:], ha[s][:, n * 512:(n + 1) * 512], ALU.add),
                     reads=[("op_ya", x), ("op_ha", s)], writes=[("op_yo", s, n)])
            S.dma("sp", K.ypre[i * 128:(i + 1) * 128, :], yo[s][:], ("op_yst", s), reads=[("op_yo", s, n) for n in range(4)], writes=[("ypre", i)])
        load_ob(0)
        load_ob(1)
        load_ob(2)
        load_hin(0)
        load_hin(1)
        prepA(0)
        prepA(1)
        prepB(0)
        for i in range(NB):
            if i + 3 < NB:
                load_ob(i + 3)
            if i + 2 < NB:
                load_hin(i + 2)
                prepA(i + 2)
            if i + 1 < NB:
                prepB(i + 1)
            main(i)
    S.barrier()


def phase_ffn_up(K, l):
    S, nc = K.S, K.nc
    with ExitStack() as st:
        sb = lambda sh, dt: _alloc(nc, st, "sb", sh, dt)
        hTt = [sb([128, KC, 512], BF16) for _ in range(2)]
        wg = [sb([128, KC, 128], BF16) for _ in range(3)]
        wv = [sb([128, KC, 128], BF16) for _ in range(3)]
        ug = [sb([128, 2 + 512], F32) for _ in range(2)]
        uv = [sb([128, 2 + 512], F32) for _ in range(2)]
        cg = [sb([128, 512], F32) for _ in range(2)]
        cv = [sb([128, 512], F32) for _ in range(2)]
        sg = [sb([128, 512], F32) for _ in range(2)]
        at = [sb([128, 512], BF16) for _ in range(3)]
        halo = sb([128, 88, 2], F32)
        pg = [_alloc(nc, st, "ps", [128, 512], F32) for _ in range(2)]
        pv = [_alloc(nc, st, "ps", [128, 512], F32) for _ in range(2)]
        V = K.vec[l]
        S.op("pool", lambda h: h.memset(halo[:], 0.0), writes=[("fu_halo", j) for j in range(NHC)])
        tiles = token_tiles()
        seq = [(ti, j) for ti in range(len(tiles)) for j in range(NHC)]
        wkeys = K.cast_keys(l, "wup", 88 * 128)

        def load_w(n):
            ti, j = seq[n]
            S.dma("sp", wg[n % 3][:], K.w_up_b[l][j * 128:(j + 1) * 128, :].rearrange("p (k j) -> p k j", k=KC),
                  ("fu_wg", n % 3), reads=wkeys, writes=[("fu_wg", n % 3)])
            S.dma("sp", wv[n % 3][:], K.w_up_b[l][(NHC + j) * 128:(NHC + j + 1) * 128, :].rearrange("p (k j) -> p k j", k=KC),
                  ("fu_wv", n % 3), reads=wkeys, writes=[("fu_wv", n % 3)])
        load_w(0)
        load_w(1)
        n = 0
        fcw = lambda c, k: V[:, V_FCW + c * 3 + k:V_FCW + c * 3 + k + 1]
        fcb = lambda c: V[:, V_FCB + c:V_FCB + c + 1]
        for ti, (t0, tw) in enumerate(tiles):
            ts = ti % 2
            S.dma("sp", hTt[ts][:, :, 0:tw], K.hT[:, :, t0:t0 + tw].rearrange("k p t -> p k t"), ("fu_h", ts),
                  reads=[("hT", ti)], writes=[("fu_h", ts)])
            for j in range(NHC):
                if n + 2 < len(seq):
                    load_w(n + 2)
                x = n % 2
                w3 = n % 3
                for (pp, ww, nm) in ((pg, wg, "fu_wg"), (pv, wv, "fu_wv")):
                    for kc in range(KC):
                        S.op("pe", lambda h, pp=pp, ww=ww, x=x, kc=kc, ts=ts, tw=tw, w3=w3: h.matmul(
                            pp[x][:, 0:tw], ww[w3][:, kc, :], hTt[ts][:, kc, 0:tw], start=(kc == 0), stop=(kc == KC - 1)),
                            reads=[(nm, w3), ("fu_h", ts)], writes=[(nm + "_p", x)])
                S.op("act", lambda h, x=x, tw=tw: h.copy(ug[x][:, 2:2 + tw], pg[x][:, 0:tw]), reads=[("fu_wg_p", x)], writes=[("fu_ug", x)])
                S.op("act", lambda h, x=x, tw=tw: h.copy(uv[x][:, 2:2 + tw], pv[x][:, 0:tw]), reads=[("fu_wv_p", x)], writes=[("fu_uv", x)])
                S.op("pool", lambda h, x=x, j=j: h.tensor_copy(ug[x][:, 0:2], halo[:, j, :]), reads=[("fu_halo", j)], writes=[("fu_ugh", x)])
                S.op("pool", lambda h, x=x, j=j: h.tensor_copy(uv[x][:, 0:2], halo[:, NHC + j, :]), reads=[("fu_halo", j)], writes=[("fu_uvh", x)])
                S.op("pool", lambda h, x=x, j=j, tw=tw: h.tensor_copy(halo[:, j, :], ug[x][:, tw:tw + 2]), reads=[("fu_ug", x), ("fu_ugh", x)], writes=[("fu_halo", j)])
                S.op("pool", lambda h, x=x, j=j, tw=tw: h.tensor_copy(halo[:, NHC + j, :], uv[x][:, tw:tw + 2]), reads=[("fu_uv", x), ("fu_uvh", x)], writes=[("fu_halo", j)])
                for (uu, cc, ch, nm) in ((ug, cg, j, "g"), (uv, cv, NHC + j, "v")):
                    S.op("dve", lambda h, uu=uu, cc=cc, ch=ch, x=x, tw=tw: h.tensor_scalar(
                        cc[x][:, 0:tw], uu[x][:, 2:2 + tw], fcw(ch, 2), fcb(ch), ALU.mult, ALU.add),
                        reads=[("fu_u" + nm, x), ("vec", l)], writes=[("fu_c" + nm, x)])
                    for k in (1, 0):
                        S.op("dve", lambda h, uu=uu, cc=cc, ch=ch, x=x, tw=tw, k=k: h.scalar_tensor_tensor(
                            out=cc[x][:, 0:tw], in0=uu[x][:, k:k + tw], scalar=fcw(ch, k), in1=cc[x][:, 0:tw], op0=ALU.mult, op1=ALU.add),
                            reads=[("fu_u" + nm, x), ("fu_u" + nm + "h", x), ("fu_c" + nm, x), ("vec", l)], writes=[("fu_c" + nm, x)])
                S.op("act", lambda h, x=x, tw=tw: h.activation(out=sg[x][:, 0:tw], in_=cg[x][:, 0:tw], func=AF.Silu),
                     reads=[("fu_cg", x)], writes=[("fu_sg", x)])
                S.op("pool", lambda h, x=x, w3=w3, tw=tw: h.tensor_tensor(at[w3][:, 0:tw], sg[x][:, 0:tw], cv[x][:, 0:tw], ALU.mult),
                     reads=[("fu_sg", x), ("fu_cv", x)], writes=[("fu_at", w3)])
                S.dma("sp", K.aT[j, :, t0:t0 + tw], at[w3][:, 0:tw], ("fu_at", w3), reads=[("fu_at", w3)], writes=[("aT", j, ti)])
                n += 1
    S.barrier()


def phase_ffn_down(K, l):
    S, nc = K.S, K.nc
    NW = 512
    with ExitStack() as st:
        sb = lambda sh, dt: _alloc(nc, st, "sb", sh, dt)
        at = [sb([128, NHC, 512], BF16) for _ in range(2)]
        wd = [sb([128, NHC, NW], BF16) for _ in range(2)]
        hin = [sb([128, NW], F32) for _ in range(2)]
        yo = [sb([128, NW], F32) for _ in range(2)]
        pd = [_alloc(nc, st, "ps", [128, NW], F32) for _ in range(4)]
        tiles = token_tiles()
        wkeys = K.cast_keys(l, "wdn", FH)
        nchunk = D // NW
        seq = [(ti, n) for ti in range(len(tiles)) for n in range(nchunk)]

        def load_w(q):
            ti, n = seq[q]
            S.dma("sp", wd[q % 2][:], K.w_down_b[l].rearrange("(p a) n -> p (a n)", a=NHC).rearrange("p (j n) -> p j n", j=NHC)[:, :, n * NW:(n + 1) * NW],
                  ("fd_wd", q % 2), reads=wkeys, writes=[("fd_wd", q % 2)])

        def load_at(ti):
            t0, tw = tiles[ti]
            S.dma("sp", at[ti % 2][:, :, 0:tw], K.aT[:, :, t0:t0 + tw].rearrange("j p t -> p j t"), ("fd_at", ti % 2),
                  reads=[("aT", j, ti) for j in range(NHC)], writes=[("fd_at", ti % 2)])
        load_at(0)
        load_w(0)
        q = 0
        m = 0
        for ti, (t0, tw) in enumerate(tiles):
            ts = ti % 2
            if ti + 1 < len(tiles):
                load_at(ti + 1)
            for n in range(nchunk):
                if q + 1 < len(seq):
                    load_w(q + 1)
                for bj in range(tw // 128):
                    i = t0 // 128 + bj
                    x = m % 4
                    y = m % 2
                    m += 1
                    S.dma("act", hin[y][:], K.h[i * 128:(i + 1) * 128, n * NW:(n + 1) * NW], ("fd_hin", y), reads=[("h", i)], writes=[("fd_hin", y)])
                    for hc in range(NHC):
                        S.op("pe", lambda h, x=x, ts=ts, hc=hc, bj=bj, q=q: h.matmul(
                            pd[x][:], at[ts][:, hc, bj * 128:(bj + 1) * 128], wd[q % 2][:, hc, :], start=(hc == 0), stop=(hc == NHC - 1)),
                            reads=[("fd_at", ts), ("fd_wd", q % 2)], writes=[("fd_pd", x)])
                    S.op("dve", lambda h, x=x, y=y: h.scalar_tensor_tensor(out=yo[y][:], in0=hin[y][:], scalar=ALPHA, in1=pd[x][:],
                                                                            op0=ALU.mult, op1=ALU.add),
                         reads=[("fd_pd", x), ("fd_hin", y)], writes=[("fd_yo", y)])
                    S.dma("act", K.ypre[i * 128:(i + 1) * 128, n * NW:(n + 1) * NW], yo[y][:], ("fd_yo", y), reads=[("fd_yo", y)], writes=[("ypre", i)])
                q += 1
    S.barrier()


def build_nc(nlayers=DEPTH, debug=False, stop=None):
    nc = bass.Bass("TRN2", target_bir_lowering=False)
    K = Ctx()
    K.nc = nc
    K.nlayers = nlayers
    dt_in = lambda name, shape: nc.dram_tensor(name, list(shape), F32, kind="ExternalInput").ap()
    kind_dbg = "ExternalOutput" if debug else "Internal"
    scr = lambda name, shape, dt: nc.dram_tensor(name, list(shape), dt, kind=kind_dbg).ap()
    K.x = dt_in("x", [SEQ, D])
    K.meta = dt_in("meta", [NMETA, D])
    K.lng = dt_in("lng", [9, D])
    K.lnb = dt_in("lnb", [9, D])
    NL = nlayers
    K.w_in = dt_in("w_in_t", [NL, NIN * 128, KC * 128])
    K.w_out = dt_in("w_out_t", [NL, 2048, 2048])
    K.w_up = dt_in("w_up_t", [NL, 88 * 128, KC * 128])
    K.w_down = dt_in("w_down_t", [NL, FH, 2048])
    K.lru_wa = dt_in("lru_wa", [NL, H, 128, 128])
    K.lru_wi = dt_in("lru_wi", [NL, H, 128, 128])
    K.vecd = dt_in("vecp", [NL, 128, NV])
    K.out = nc.dram_tensor("out", [SEQ, D], F32, kind="ExternalOutput").ap()
    wsc = lambda name, shape: nc.dram_tensor(name, list(shape), BF16, kind="Internal").ap()
    K.w_in_b = wsc("w_in_b", [NL, NIN * 128, KC * 128])
    K.w_out_b = wsc("w_out_b", [NL, 2048, 2048])
    K.w_up_b = wsc("w_up_b", [NL, 88 * 128, KC * 128])
    K.w_down_b = wsc("w_down_b", [NL, FH, 2048])
    K.h = scr("h_s", [TP, D], F32)
    K.ypre = scr("ypre_s", [TP, D], F32)
    K.hT = scr("hT_s", [KC, 128, TP], BF16)
    K.qkvT = scr("qkvT_s", [24, 128, TP], BF16)
    K.pf = scr("pf_s", [16, 128, TP], F32)
    K.oT = scr("oT_s", [16, 128, TP], F32)
    K.aT = scr("aT_s", [NHC, 128, TP], BF16)
    K.w_out = K.w_out
    with ExitStack() as st:
        S = Sched(nc, st)
        K.S = S
        sb = lambda sh, dt: _alloc(nc, st, "sb", sh, dt)
        K.ones = sb([128, 512], F32)
        K.onesf2 = sb([128, 2], F32)
        identf = sb([128, 128], F32)
        K.ident = sb([128, 128], BF16)
        K.mtrif = sb([128, 128], F32)
        K.m0f = sb([128, 128], F32)
        K.vec = [sb([128, NV], F32) for _ in range(nlayers)]
        S.op("pool", lambda h: h.memset(K.ones[:], 1.0), writes=["ones"])
        S.op("pool", lambda h: h.memset(K.onesf2[:], 1.0), writes=["ones"])
        S.op("pool", lambda h: h.memset(identf[:], 0.0), writes=["identf"])
        S.op("pool", lambda h: h.affine_select(out=identf[:], in_=K.ones[:, 0:128], pattern=[[-1, 128]], compare_op=ALU.is_equal,
                                                fill=0.0, base=0, channel_multiplier=1), reads=["ones"], writes=["identf"])
        S.op("dve", lambda h: h.tensor_copy(K.ident[:], identf[:]), reads=["identf"], writes=["ident"])
        S.op("pool", lambda h: h.affine_select(out=K.mtrif[:], in_=K.ones[:, 0:128], pattern=[[-1, 128]], compare_op=ALU.is_gt,
                                                fill=0.0, base=0, channel_multiplier=1), reads=["ones"], writes=["masks"])
        S.op("pool", lambda h: h.tensor_copy(K.m0f[:], K.mtrif[:]), reads=["masks"], writes=["masks0"])
        S.op("pool", lambda h: h.memset(K.m0f[:, 0:NPAD], 0.0), reads=["masks0"], writes=["masks0"])
        K.zeros = sb([128, 512], F32)
        K.imtrif = sb([128, 128], F32)
        K.im0f = sb([128, 128], F32)
        S.op("pool", lambda h: h.memset(K.zeros[:], 0.0), writes=["zeros"])
        S.op("dve", lambda h: h.tensor_scalar(K.imtrif[:], K.mtrif[:], -1.0, 1.0, ALU.mult, ALU.add), reads=["masks"], writes=["imasks"])
        S.op("dve", lambda h: h.tensor_scalar(K.im0f[:], K.m0f[:], -1.0, 1.0, ALU.mult, ALU.add), reads=["masks0"], writes=["imasks"])
        for l in range(nlayers):
            S.dma("sp", K.vec[l][:], K.vecd[l], ("vecld", l), writes=[("vec", l)])
        S.barrier()
        phase_cast(K, [0])

        def src0(S_, i, tile, key):
            if i == 0:
                S_.op("pool", lambda h: h.memset(tile[:], 0.0), writes=[key])
                S_.dma("sp", tile[NPAD:128, :], K.meta, ("ln_yld", key), writes=[key])
            else:
                S_.dma("sp", tile[:], K.x[(i - 1) * 128:i * 128, :], ("ln_yld", key), writes=[key])

        def srcy(S_, i, tile, key):
            S_.dma("sp", tile[:], K.ypre[i * 128:(i + 1) * 128, :], ("ln_yld", key), reads=[("ypre", i)], writes=[key])
        dsth = lambda i: K.h[i * 128:(i + 1) * 128, :]
        dstout = lambda i: (None if i == 0 else K.out[(i - 1) * 128:i * 128, :])
        phases = []
        phases.append(("ln0", lambda: phase_ln(K, 0, src0, dsth)))
        for l in range(nlayers):
            last = (l == nlayers - 1)
            phases.append(("inproj%d" % l, lambda l=l: phase_inproj(K, l)))
            phases.append(("attn%d" % l, lambda l=l: phase_attn(K, l)))
            if l == 0 and nlayers > 1:
                phases.append(("cast_rest", lambda: phase_cast(K, list(range(1, nlayers)))))
            phases.append(("lru%d" % l, lambda l=l: phase_lru(K, l)))
            phases.append(("outproj%d" % l, lambda l=l: phase_outproj(K, l)))
            phases.append(("ln1_%d" % l, lambda l=l: phase_ln(K, 1 + 2 * l, srcy, dsth)))
            phases.append(("ffnup%d" % l, lambda l=l: phase_ffn_up(K, l)))
            phases.append(("ffndn%d" % l, lambda l=l: phase_ffn_down(K, l)))
            if last and not debug:
                phases.append(("ln2_%d" % l, lambda l=l: phase_ln(K, 2 + 2 * l, srcy, dstout, write_hT=False)))
            else:
                phases.append(("ln2_%d" % l, lambda l=l: phase_ln(K, 2 + 2 * l, srcy, dsth)))
        for name, fn in phases:
            fn()
            if stop is not None and name == stop:
                break
        S.finish()
        K.ninstr = dict(S.ninstr)
        K.nsem = S.nsem
    return nc, K


def host_layout(inputs, nl=DEPTH):
    f = lambda a: np.ascontiguousarray(np.asarray(a, dtype=np.float32))
    w_in = f(inputs["w_in"])
    w_in_t = np.ascontiguousarray(w_in.reshape(DEPTH, KC, 128, NIN, 128).transpose(0, 3, 2, 1, 4)).reshape(DEPTH, NIN * 128, KC * 128)
    w_out = f(inputs["w_out"])
    w_out_t = np.ascontiguousarray(w_out.reshape(DEPTH, KC, 128, D).transpose(0, 2, 1, 3)).reshape(DEPTH, 128, KC * D).reshape(DEPTH, 2048, 2048)
    w_up = f(inputs["ffn_w_up"])
    w_up_t = np.ascontiguousarray(w_up.reshape(DEPTH, KC, 128, 88, 128).transpose(0, 3, 2, 1, 4)).reshape(DEPTH, 88 * 128, KC * 128)
    w_dn = f(inputs["ffn_w_down"])
    w_dn_t = np.ascontiguousarray(w_dn.reshape(DEPTH, NHC, 128, D).transpose(0, 2, 1, 3)).reshape(DEPTH, 128, NHC * D).reshape(DEPTH, FH, 2048)
    vec = np.zeros((DEPTH, 128, NV), np.float32)
    pc = lambda v, n: v.reshape(DEPTH, n, 128).transpose(0, 2, 1)
    vec[:, :, V_CW:V_CW + 32] = f(inputs["lru_conv_w"]).reshape(DEPTH, 4, 8, 128).transpose(0, 3, 2, 1).reshape(DEPTH, 128, 32)
    vec[:, :, V_CB:V_CB + 8] = pc(f(inputs["lru_conv_b"]), 8)
    vec[:, :, V_BA:V_BA + 8] = pc(f(inputs["lru_b_a"]), 8)
    vec[:, :, V_BI:V_BI + 8] = pc(f(inputs["lru_b_i"]), 8)
    vec[:, :, V_LAM:V_LAM + 8] = pc(f(inputs["lru_lambda"]), 8)
    gcat = np.concatenate([f(inputs["g_attn"]), f(inputs["g_lru"])], axis=1)
    vec[:, :, V_G:V_G + 16] = pc(gcat, 16)
    vec[:, :, V_FCW:V_FCW + 264] = f(inputs["ffn_conv_w"]).reshape(DEPTH, 3, 88, 128).transpose(0, 3, 2, 1).reshape(DEPTH, 128, 264)
    vec[:, :, V_FCB:V_FCB + 88] = pc(f(inputs["ffn_conv_b"]), 88)
    lng = np.zeros((9, D), np.float32)
    lnb = np.zeros((9, D), np.float32)
    lng[0] = f(inputs["ln0_g"])
    lnb[0] = f(inputs["ln0_b"])
    for l in range(DEPTH):
        lng[1 + 2 * l] = f(inputs["ln1_g"])[l]
        lnb[1 + 2 * l] = f(inputs["ln1_b"])[l]
        lng[2 + 2 * l] = f(inputs["ln2_g"])[l]
        lnb[2 + 2 * l] = f(inputs["ln2_b"])[l]
    shared = {
        "meta": f(inputs["meta_tokens"]), "lng": lng, "lnb": lnb,
        "w_in_t": w_in_t[:nl], "w_out_t": w_out_t[:nl], "w_up_t": w_up_t[:nl], "w_down_t": w_dn_t[:nl],
        "lru_wa": f(inputs["lru_w_a"])[:nl], "lru_wi": f(inputs["lru_w_i"])[:nl], "vecp": vec[:nl],
    }
    return shared


_cache = {}


def kernel(**inputs):
    x = np.asarray(inputs["x"], dtype=np.float32)
    B = x.shape[0]
    shared = host_layout(inputs)
    if "nc" not in _cache:
        _cache["nc"] = build_nc()[0]
    nc = _cache["nc"]
    in_maps = []
    for c in range(B):
        m = dict(shared)
        m["x"] = np.ascontiguousarray(x[c])
        in_maps.append(m)
    res = run_bass_kernel_spmd(nc, in_maps, core_ids=list(range(B)))
    out = np.stack([np.asarray(res.results[c]["out"], dtype=np.float32) for c in range(B)], axis=0)
    return out
```
